# Optimizing a Trainium2 kernel written in Bass

```python
import math
import jax, jax.numpy as jnp
from jax import lax
import numpy as np

D_MODEL = 1024
BATCH = 2
SEQ = 8192
DEPTH = 1

N_MEM = 256
EPS = 1e-6
S5_WIDTH = D_MODEL // 2
S5_GROUP = 16
S5_GROUPS = S5_WIDTH // S5_GROUP
S5_STATE = 64
DT_MIN = 1e-3
DT_MAX = 1e-1
M_WIDTH = D_MODEL // 2
M_HEADS = 4
M_HEAD_DIM = M_WIDTH // M_HEADS
M_CONV = 4
M_CHUNK = 128
X_HEADS = 4
X_HEAD_DIM = D_MODEL // X_HEADS
D_FF = ((8 * D_MODEL // 3 + 255) // 256) * 256
FFN_CONV = 3
IN_WIDTHS = (S5_WIDTH, M_WIDTH, M_WIDTH, M_WIDTH, M_WIDTH, M_HEADS, M_HEADS, 2 * D_MODEL)
IN_WIDTH = sum(IN_WIDTHS)

kernel_name = "hybrid_s5_mlstm_gated_xattn_convffn"


def _rmsnorm(x, g):
    xf = x.astype(jnp.float32)
    y = xf * lax.rsqrt(jnp.mean(xf * xf, axis=-1, keepdims=True) + EPS)
    return (y * g.astype(jnp.float32)).astype(x.dtype)


def _causal_dwconv(x, w, b):
    K, C = w.shape
    y = lax.conv_general_dilated(
        x, w[:, None, :].astype(x.dtype), window_strides=(1,), padding=[(K - 1, 0)],
        dimension_numbers=('NWC', 'WIO', 'NWC'), feature_group_count=C)
    return y + b.astype(x.dtype)


def _s5(u, lam_re, lam_im, b_re, b_im, c_re, c_im, d, log_dt, w_glu, b_glu):
    Bsz, L, _ = u.shape
    f32 = jnp.float32
    lam = lax.complex(lam_re.astype(f32), lam_im.astype(f32))
    dt = jnp.exp(log_dt.astype(f32))[:, None]
    a_bar = jnp.exp(lam * dt)
    b_mat = lax.complex(b_re.astype(f32), b_im.astype(f32))
    b_bar = ((a_bar - 1.0) / lam)[..., None] * b_mat
    ug = u.astype(f32).reshape(Bsz, L, S5_GROUPS, S5_GROUP)
    bu = jnp.einsum('blgh,gph->blgp', ug.astype(jnp.complex64), b_bar)
    a_seq = jnp.broadcast_to(a_bar, bu.shape)

    def combine(left, right):
        a_l, s_l = left
        a_r, s_r = right
        return a_r * a_l, a_r * s_l + s_r

    _, states = lax.associative_scan(combine, (a_seq, bu), axis=1)
    c_mat = lax.complex(c_re.astype(f32), c_im.astype(f32))
    y = jnp.einsum('blgp,ghp->blgh', states, c_mat).real + d.astype(f32) * ug
    y = jax.nn.gelu(y.reshape(Bsz, L, S5_WIDTH))
    y = y * jax.nn.sigmoid(y @ w_glu.astype(f32) + b_glu.astype(f32))
    return y.astype(u.dtype)


def _mlstm(q_in, k_in, v_in, o_in, i_pre, f_pre, conv_w, conv_b, b_i, b_f, norm_g):
    Bsz, L, _ = v_in.shape
    H, Dh, Lc = M_HEADS, M_HEAD_DIM, M_CHUNK
    NC = L // Lc
    f32 = jnp.float32
    qk = jax.nn.silu(_causal_dwconv(jnp.concatenate([q_in, k_in], axis=-1), conv_w, conv_b))
    q_raw, k_raw = jnp.split(qk, 2, axis=-1)

    def heads(t):
        return t.astype(f32).reshape(Bsz, NC, Lc, H, Dh).transpose(0, 3, 1, 2, 4)

    def gate(t, b):
        return (t.astype(f32) + b.astype(f32)).reshape(Bsz, NC, Lc, H).transpose(0, 3, 1, 2)

    q = heads(q_raw)
    k = heads(k_raw) * (Dh ** -0.5)
    v = heads(v_in)
    ig = gate(i_pre, b_i)
    logf = jax.nn.log_sigmoid(gate(f_pre, b_f))
    bcum = jnp.cumsum(logf, axis=-1)
    g_chunk = bcum[..., -1]
    a_end = g_chunk[..., None] - bcum + ig
    m_loc = jnp.max(a_end, axis=-1)

    def step(carry, xs):
        C, n, m = carry
        k_c, v_c, a_c, g_c, ml_c = xs
        m_new = jnp.maximum(g_c + m, ml_c)
        decay = jnp.exp(g_c + m - m_new)
        w = jnp.exp(a_c - m_new[..., None])
        C_new = decay[..., None, None] * C + jnp.einsum('bhs,bhsk,bhsv->bhkv', w, k_c, v_c)
        n_new = decay[..., None] * n + jnp.einsum('bhs,bhsk->bhk', w, k_c)
        return (C_new, n_new, m_new), (C, n, m)

    init = (jnp.zeros((Bsz, H, Dh, Dh), f32), jnp.zeros((Bsz, H, Dh), f32), jnp.zeros((Bsz, H), f32))
    xs = (k.transpose(2, 0, 1, 3, 4), v.transpose(2, 0, 1, 3, 4), a_end.transpose(2, 0, 1, 3),
          g_chunk.transpose(2, 0, 1), m_loc.transpose(2, 0, 1))
    _, (C_prev, n_prev, m_prev) = lax.scan(step, init, xs)
    C_prev = C_prev.transpose(1, 2, 0, 3, 4)
    n_prev = n_prev.transpose(1, 2, 0, 3)
    m_prev = m_prev.transpose(1, 2, 0)

    inter = bcum + m_prev[..., None]
    causal = jnp.tril(jnp.ones((Lc, Lc), dtype=bool))
    dlog = bcum[..., :, None] - bcum[..., None, :] + ig[..., None, :]
    dlog = jnp.where(causal, dlog, -jnp.inf)
    m_t = jnp.maximum(inter, jnp.max(dlog, axis=-1))
    s = jnp.einsum('bhctd,bhcsd->bhcts', q, k) * jnp.exp(dlog - m_t[..., None])
    inter_w = jnp.exp(inter - m_t)
    num = (inter_w[..., None] * jnp.einsum('bhctk,bhckv->bhctv', q, C_prev)
           + jnp.einsum('bhcts,bhcsv->bhctv', s, v))
    den = inter_w * jnp.einsum('bhctk,bhck->bhct', q, n_prev) + jnp.sum(s, axis=-1)
    h = num / jnp.maximum(jnp.abs(den), jnp.exp(-m_t))[..., None]
    h = h.transpose(0, 2, 3, 1, 4).reshape(Bsz, L, H, Dh)
    h = jax.nn.sigmoid(o_in.astype(f32)).reshape(Bsz, L, H, Dh) * h
    h = h * lax.rsqrt(jnp.mean(h * h, axis=-1, keepdims=True) + EPS)
    h = h * norm_g.astype(f32).reshape(H, Dh)
    return h.reshape(Bsz, L, M_WIDTH).astype(v_in.dtype)


def _cross_attn(h, mem_n, wq, wkv, wo):
    Bsz, L, _ = h.shape
    M = mem_n.shape[1]
    q = (h @ wq).reshape(Bsz, L, X_HEADS, X_HEAD_DIM)
    k, v = jnp.split(mem_n @ wkv, 2, axis=-1)
    k = k.reshape(Bsz, M, X_HEADS, X_HEAD_DIM)
    v = v.reshape(Bsz, M, X_HEADS, X_HEAD_DIM)
    s = jnp.einsum('blhd,bmhd->bhlm', q, k).astype(jnp.float32) * (X_HEAD_DIM ** -0.5)
    p = jax.nn.softmax(s, axis=-1).astype(h.dtype)
    o = jnp.einsum('bhlm,bmhd->blhd', p, v).reshape(Bsz, L, D_MODEL)
    return o @ wo


def _conv_ffn(h, w_up, conv_w, conv_b, w_down):
    u = _causal_dwconv(h @ w_up, conv_w, conv_b)
    a, b = jnp.split(u, 2, axis=-1)
    return (jax.nn.gelu(a) * b) @ w_down


def setup_inputs(seed: int = 0) -> dict:
    key = jax.random.key(seed)
    ks = iter(jax.random.split(key, 40))
    f32 = jnp.float32

    def nrm(shape, scale):
        return jax.random.normal(next(ks), shape, f32) * scale

    def gain(shape):
        return 1.0 + nrm(shape, 0.02)

    L_, G, P, Hs = DEPTH, S5_GROUPS, S5_STATE, S5_GROUP
    lam_im_base = jnp.pi * jnp.arange(P, dtype=f32)
    f_bias_base = jnp.linspace(3.0, 6.0, M_HEADS, dtype=f32)
    return {
        "x": nrm((BATCH, SEQ, D_MODEL), 1.0),
        "mem": nrm((BATCH, N_MEM, D_MODEL), 1.0),
        "mix_norm_g": gain((L_, D_MODEL)),
        "w_in": nrm((L_, D_MODEL, IN_WIDTH), D_MODEL ** -0.5),
        "s5_lam_re": -0.5 + nrm((L_, G, P), 0.01),
        "s5_lam_im": lam_im_base + nrm((L_, G, P), 0.01),
        "s5_b_re": nrm((L_, G, P, Hs), (2 * Hs) ** -0.5),
        "s5_b_im": nrm((L_, G, P, Hs), (2 * Hs) ** -0.5),
        "s5_c_re": nrm((L_, G, Hs, P), (2 * P) ** -0.5),
        "s5_c_im": nrm((L_, G, Hs, P), (2 * P) ** -0.5),
        "s5_d": nrm((L_, G, Hs), 1.0),
        "s5_log_dt": jax.random.uniform(next(ks), (L_, G), f32, math.log(DT_MIN), math.log(DT_MAX)),
        "s5_w_glu": nrm((L_, S5_WIDTH, S5_WIDTH), S5_WIDTH ** -0.5),
        "s5_b_glu": nrm((L_, S5_WIDTH), 0.01),
        "m_conv_w": nrm((L_, M_CONV, 2 * M_WIDTH), M_CONV ** -0.5),
        "m_conv_b": nrm((L_, 2 * M_WIDTH), 0.01),
        "m_b_i": nrm((L_, M_HEADS), 0.1),
        "m_b_f": f_bias_base + nrm((L_, M_HEADS), 0.1),
        "m_norm_g": gain((L_, M_WIDTH)),
        "w_br_s5": nrm((L_, S5_WIDTH, D_MODEL), S5_WIDTH ** -0.5),
        "w_br_m": nrm((L_, M_WIDTH, D_MODEL), M_WIDTH ** -0.5),
        "b_gate": nrm((L_, 2 * D_MODEL), 0.01),
        "w_out": nrm((L_, D_MODEL, D_MODEL), D_MODEL ** -0.5),
        "x_norm_g": gain((L_, D_MODEL)),
        "mem_norm_g": gain((L_, D_MODEL)),
        "x_wq": nrm((L_, D_MODEL, D_MODEL), D_MODEL ** -0.5),
        "x_wkv": nrm((L_, D_MODEL, 2 * D_MODEL), D_MODEL ** -0.5),
        "x_wo": nrm((L_, D_MODEL, D_MODEL), D_MODEL ** -0.5),
        "f_norm_g": gain((L_, D_MODEL)),
        "f_w_up": nrm((L_, D_MODEL, 2 * D_FF), D_MODEL ** -0.5),
        "f_conv_w": nrm((L_, FFN_CONV, 2 * D_FF), FFN_CONV ** -0.5),
        "f_conv_b": nrm((L_, 2 * D_FF), 0.01),
        "f_w_down": nrm((L_, D_FF, D_MODEL), D_FF ** -0.5),
        "final_norm_g": gain((D_MODEL,)),
    }


def reference(x, mem, mix_norm_g, w_in, s5_lam_re, s5_lam_im, s5_b_re, s5_b_im, s5_c_re, s5_c_im,
              s5_d, s5_log_dt, s5_w_glu, s5_b_glu, m_conv_w, m_conv_b, m_b_i, m_b_f, m_norm_g,
              w_br_s5, w_br_m, b_gate, w_out, x_norm_g, mem_norm_g, x_wq, x_wkv, x_wo,
              f_norm_g, f_w_up, f_conv_w, f_conv_b, f_w_down, final_norm_g):
    split_points = [sum(IN_WIDTHS[:i + 1]) for i in range(len(IN_WIDTHS) - 1)]
    for l in range(DEPTH):
        h = _rmsnorm(x, mix_norm_g[l])
        proj = h @ w_in[l]
        u_s5, q_in, k_in, v_in, o_in, i_pre, f_pre, gate_pre = jnp.split(proj, split_points, axis=-1)
        y_s5 = _s5(u_s5, s5_lam_re[l], s5_lam_im[l], s5_b_re[l], s5_b_im[l], s5_c_re[l], s5_c_im[l],
                   s5_d[l], s5_log_dt[l], s5_w_glu[l], s5_b_glu[l])
        y_m = _mlstm(q_in, k_in, v_in, o_in, i_pre, f_pre, m_conv_w[l], m_conv_b[l],
                     m_b_i[l], m_b_f[l], m_norm_g[l])
        gates = jax.nn.sigmoid((gate_pre + b_gate[l]).astype(jnp.float32)).astype(x.dtype)
        g_s5, g_m = jnp.split(gates, 2, axis=-1)
        merged = g_s5 * (y_s5 @ w_br_s5[l]) + g_m * (y_m @ w_br_m[l])
        x = x + merged @ w_out[l]
        h = _rmsnorm(x, x_norm_g[l])
        x = x + _cross_attn(h, _rmsnorm(mem, mem_norm_g[l]), x_wq[l], x_wkv[l], x_wo[l])
        h = _rmsnorm(x, f_norm_g[l])
        x = x + _conv_ffn(h, f_w_up[l], f_conv_w[l], f_conv_b[l], f_w_down[l])
    return _rmsnorm(x, final_norm_g)
```

```python
import contextlib
import math
import numpy as np
import concourse.bass as bass
import concourse.mybir as mybir
from concourse.bass_utils import run_bass_kernel_spmd

F32 = mybir.dt.float32
BF16 = mybir.dt.bfloat16
I32 = mybir.dt.int32
AF = mybir.ActivationFunctionType
ALU = mybir.AluOpType
AX = mybir.AxisListType

ENGS = ("tensor", "vector", "scalar", "gpsimd", "sync")
D = 1024
SEG = 2048
NPRE = 6144
NHALO = 128
NTOK = NPRE + NHALO + SEG
NCHT = NTOK // 128
GC = math.sqrt(2.0 / math.pi)


class Buf:
    __slots__ = ("name", "w", "r", "excl")

    def __init__(self, name, excl=False):
        self.name = name
        self.w = None
        self.r = {}
        self.excl = excl


class T:
    def __init__(self, t, name, excl=False):
        self.t = t
        self.b = Buf(name, excl)
        self.subs = None

    def __getitem__(self, k):
        return self.t[k]

    def split(self, keys):
        self.subs = {k: Buf("%s_%s" % (self.b.name, k)) for k in keys}
        return self

    def s(self, *key):
        if len(key) == 1 and key[0] in self.subs:
            return [self.subs[key[0]]]
        return [b for k, b in self.subs.items() if isinstance(k, tuple) and k[:len(key)] == key]


class TV:
    def __init__(self, parent, ap):
        self.t = ap
        self.b = parent.b

    def __getitem__(self, k):
        return self.t[k]


def _bufs(lst):
    out = []
    for x in lst:
        if x is None:
            continue
        if isinstance(x, T) and x.subs:
            out.extend(x.subs.values())
        elif isinstance(x, (T, TV)):
            out.append(x.b)
        elif isinstance(x, (list, tuple)):
            out.extend(_bufs(x))
        else:
            out.append(x)
    return out


class MK:
    def __init__(self, nc, n_dma_ch=16):
        self.nc = nc
        self.ops = {e: [] for e in ENGS}
        self.cnt = {}
        self.known = {e: {} for e in ENGS}
        self.sems = {}
        self._ctx = []
        for e in ENGS:
            self._mk_sem("E_" + e)
        self.n_dma_ch = n_dma_ch
        self.snap = {}
        self.dma_rr = {e: 0 for e in ENGS}
        for e in ("sync", "gpsimd"):
            for c in range(n_dma_ch):
                self._mk_sem("D_%s_%d" % (e, c))

    def _mk_sem(self, key):
        cm = self.nc.semaphore(key)
        h = cm.__enter__()
        self._ctx.append(cm)
        self.sems[key] = h
        self.cnt[key] = 0

    def _need(self, eng, key, val, waits):
        if val <= 0 or self.known[eng].get(key, 0) >= val:
            return
        waits[key] = max(waits.get(key, 0), val)

    def _emit_waits(self, eng, waits):
        kn = self.known[eng]
        for k, v in sorted(waits.items(), key=lambda kv: (kv[0] == "E_tensor", kv[0])):
            if kn.get(k, 0) >= v:
                continue
            kn[k] = v
            h = self.sems[k]
            self.ops[eng].append(lambda E, h=h, v=v: E.wait_ge(h, v))
            s = self.snap.get((k, v))
            if s:
                for kk, vv in s.items():
                    if kn.get(kk, 0) < vv:
                        kn[kk] = vv

    def _deps(self, eng, reads, writes, waits, is_dma=False, xr=()):
        own = "E_" + eng
        for b in reads:
            for k, v in (b.w or {}).items():
                if k == own and not is_dma:
                    if eng == "tensor":
                        continue
                    if id(b) in xr:
                        continue
                    if eng in ("vector", "scalar") and v < self.cnt[own]:
                        continue
                self._need(eng, k, v, waits)
        for b in writes:
            for k, v in (b.w or {}).items():
                if k == own and not is_dma:
                    continue
                if is_dma and k.startswith("D_") and not b.r:
                    continue
                self._need(eng, k, v, waits)
            for k, v in b.r.items():
                if k != own or is_dma:
                    self._need(eng, k, v, waits)

    def _mark(self, key, val, reads, writes, is_dma=False):
        for b in reads:
            if b not in writes:
                b.r[key] = max(b.r.get(key, 0), val)
        for b in writes:
            if is_dma and b.w and not b.r:
                keep = {k: v for k, v in b.w.items() if k.startswith("D_")}
                keep[key] = val
                b.w = keep
            else:
                b.w = {key: val}
            b.r = {}

    def _split(self, reads, writes):
        reads = _bufs(reads)
        writes = _bufs(writes)
        r2 = []
        xr = set()
        for b in reads:
            if b.excl:
                if b not in writes:
                    writes.append(b)
                    xr.add(id(b))
            if b not in r2:
                r2.append(b)
        return r2, writes, xr

    def op(self, eng, fn, reads=(), writes=()):
        reads, writes, xr = self._split(reads, writes)
        waits = {}
        self._deps(eng, reads, writes, waits, xr=xr)
        self._emit_waits(eng, waits)
        key = "E_" + eng
        self.cnt[key] += 1
        val = self.cnt[key]
        h = self.sems[key]
        self.ops[eng].append(lambda E, fn=fn, h=h: fn(E).then_inc(h, 1))
        self.snap[(key, val)] = dict(self.known[eng])
        self._mark(key, val, reads, writes)

    def dma(self, eng, out, in_, reads=(), writes=(), **kw):
        reads, writes, _xr = self._split(reads, writes)
        c = self.dma_rr[eng]
        self.dma_rr[eng] = (c + 1) % self.n_dma_ch
        key = "D_%s_%d" % (eng, c)
        waits = {}
        self._need(eng, key, self.cnt[key], waits)
        self._deps(eng, reads, writes, waits, is_dma=True)
        self._emit_waits(eng, waits)
        self.cnt[key] += 16
        val = self.cnt[key]
        h = self.sems[key]
        self.ops[eng].append(
            lambda E, out=out, in_=in_, h=h, kw=kw: E.dma_start(out=out, in_=in_, **kw).then_inc(h, 16))
        self.snap[(key, val)] = dict(self.known[eng])
        self._mark(key, val, reads, writes, is_dma=True)

    def wait_bufs(self, eng, bufs):
        waits = {}
        for b in _bufs(bufs):
            for k, v in (b.w or {}).items():
                self._need(eng, k, v, waits)
        self._emit_waits(eng, waits)

    def barrier(self):
        for e in ENGS:
            waits = {}
            for k, v in self.cnt.items():
                self._need(e, k, v, waits)
            self._emit_waits(e, waits)

    def emit(self):
        nc = self.nc
        ops = self.ops
        with nc.Block() as block:
            @block.tensor
            def _(E):
                for f in ops["tensor"]:
                    f(E)

            @block.vector
            def _(E):
                for f in ops["vector"]:
                    f(E)

            @block.scalar
            def _(E):
                for f in ops["scalar"]:
                    f(E)

            @block.gpsimd
            def _(E):
                for f in ops["gpsimd"]:
                    f(E)

            @block.sync
            def _(E):
                for f in ops["sync"]:
                    f(E)
        self.ops = {e: [] for e in ENGS}

    def close(self):
        for cm in reversed(self._ctx):
            cm.__exit__(None, None, None)
        self._ctx = []


W_SPECS = [
    ("w_in", 1024, 4616), ("s5_w_glu", 512, 512), ("w_br_s5", 512, 1024), ("w_br_m", 512, 1024),
    ("w_out", 1024, 1024), ("x_wq", 1024, 1024), ("x_wkv", 1024, 2048), ("x_wo", 1024, 1024),
    ("f_w_up", 1024, 5632), ("f_w_down", 2816, 1024),
]
SMALL_SPECS = [
    ("mix_norm_g", [1024]), ("s5_lam_re", [32, 64]), ("s5_lam_im", [32, 64]), ("s5_b_re", [32, 64, 16]),
    ("s5_b_im", [32, 64, 16]), ("s5_c_re", [512, 64]), ("s5_c_im", [512, 64]), ("s5_d", [512]),
    ("s5_log_dt", [32]), ("s5_b_glu", [512]), ("m_conv_w", [4, 1024]), ("m_conv_b", [1024]),
    ("m_b_i", [4]), ("m_b_f", [4]), ("m_norm_g", [512]), ("b_gate", [2048]), ("x_norm_g", [1024]),
    ("mem_norm_g", [1024]), ("f_norm_g", [1024]), ("f_conv_w", [3, 5632]), ("f_conv_b", [5632]),
    ("final_norm_g", [1024]),
    ("c_ident", [128, 128]), ("c_tri", [128, 128]), ("c_ones", [128, 128]), ("c_mmask", [128, 128]),
    ("vmask", [128, NCHT]), ("hvalid", [128, 1]),
]


class KB:
    def __init__(self, nc, es, dbg=None):
        self.nc = nc
        self.es = es
        self.mk = MK(nc)
        self.dbg = dbg
        self.din = {}
        self.banks = []
        self.tbanks = []
        self.bank_i = 0
        self.tbank_i = 0
        self.tmpf = []
        self.tmpf_i = 0
        self.tmpb = []
        self.tmpb_i = 0
        self.wslots = []
        self.wslot_i = 0
        self.bq = "sync"

    def sb(self, name, shape, dt, es=None):
        t = (es or self.es).enter_context(self.nc.sbuf_tensor(name, shape, dt))
        return T(t, name)

    def dram(self, name, shape, dt, kind):
        return T(self.nc.dram_tensor(name, shape, dt, kind=kind).ap(), name)

    def bank(self):
        b = self.banks[self.bank_i % len(self.banks)]
        self.bank_i += 1
        return b

    def tbank(self):
        b = self.tbanks[self.tbank_i % len(self.tbanks)]
        self.tbank_i += 1
        return b

    def tf(self):
        b = self.tmpf[self.tmpf_i % len(self.tmpf)]
        self.tmpf_i += 1
        return b

    def tb(self):
        b = self.tmpb[self.tmpb_i % len(self.tmpb)]
        self.tmpb_i += 1
        return b

    def V(self, fn, r=(), w=()):
        self.mk.op("vector", fn, r, w)

    def A(self, fn, r=(), w=()):
        self.mk.op("scalar", fn, r, w)

    def G(self, fn, r=(), w=()):
        self.mk.op("gpsimd", fn, r, w)

    def P(self, fn, r=(), w=()):
        self.mk.op("tensor", fn, r, w)

    def act(self, out, in_, func, r, w, **kw):
        self.A(lambda E: E.activation(out=out, in_=in_, func=func, **kw), r, w)

    def tt(self, eng, out, in0, in1, op, r, w):
        self.mk.op(eng, lambda E: E.tensor_tensor(out=out, in0=in0, in1=in1, op=op), r, w)

    def ts(self, eng, out, in0, s1, s2, op0, op1, r, w):
        if s2 is None:
            self.mk.op(eng, lambda E: E.tensor_scalar(out=out, in0=in0, scalar1=s1, scalar2=None, op0=op0), r, w)
        else:
            self.mk.op(eng, lambda E: E.tensor_scalar(out=out, in0=in0, scalar1=s1, scalar2=s2, op0=op0, op1=op1), r, w)

    def stt(self, out, in0, sc, in1, op0, op1, r, w):
        self.V(lambda E: E.scalar_tensor_tensor(out=out, in0=in0, scalar=sc, in1=in1, op0=op0, op1=op1), r, w)

    def cp(self, eng, out, in_, r, w):
        if eng == "scalar":
            self.A(lambda E: E.copy(out=out, in_=in_), r, w)
        else:
            self.mk.op(eng, lambda E: E.tensor_copy(out=out, in_=in_), r, w)

    def mm(self, out, lhsT, rhs, start, stop, r, w):
        self.P(lambda E: E.matmul(out, lhsT=lhsT, rhs=rhs, start=start, stop=stop), r, w)

    def tr(self, out, in_, ident, r, w):
        self.P(lambda E: E.transpose(out=out, in_=in_, identity=ident), r, w)

    def dma(self, eng, out, in_, r, w, **kw):
        self.mk.dma(eng, out, in_, r, w, **kw)

    def memset(self, eng, ap, val, w):
        self.mk.op(eng, lambda E: E.memset(ap, val), (), w)

    def wslab(self, src_t, src_ap, kt, ncols, half=None, slot=None, q="sync"):
        ph = self.wphase
        if slot is not None:
            idx = slot
            self._last_slot = slot
        elif half is None or half == 0:
            idx = self.wpool[ph][self.wpool_i[ph] % len(self.wpool[ph])]
            self.wpool_i[ph] += 1
            self._last_slot = idx
        else:
            idx = self._last_slot
        s = self.wslots[idx]
        off = 0 if not half else 2048
        view = s.t[:, off:off + kt * ncols].rearrange("p (k n) -> p k n", k=kt)
        self.dma(q, view, src_ap, [src_t], [s])
        return s, view

    def declare_io(self):
        nc = self.nc
        self.xs = self.dram("xs", [NTOK, D], F32, "ExternalInput")
        self.mem = self.dram("mem", [256, D], F32, "ExternalInput")
        self.out = self.dram("out", [SEG, D], F32, "ExternalOutput")
        self.wf = {}
        self.wb = {}
        for name, r, c in W_SPECS:
            self.wf[name] = self.dram(name, [r, c], F32, "ExternalInput")
            self.wb[name] = self.dram(name + "_bf", [r, c], BF16, "Internal")
        for name, shape in SMALL_SPECS:
            self.din[name] = self.dram(name, shape, F32, "ExternalInput")
        self.scrA = self.dram("scrA", [512, 8, 64], BF16, "Internal")
        self.scrB = self.dram("scrB", [512, 8, 64], BF16, "Internal")

    def precast_cols(self, name, blocks, defer=False):
        lst = self.wblocks.setdefault(name, [])
        for (c0, c1) in blocks:
            bb = Buf("%s_%d" % (name, c0))
            lst.append((c0, c1, bb))
            if defer:
                self.pc_jobs.append((name, c0, c1, bb))
            else:
                self.dma("gpsimd", self.wb[name][:, c0:c1], self.wf[name][:, c0:c1], [], [bb])

    def precast_flush(self, n, after=()):
        for _ in range(n):
            if not self.pc_jobs:
                return
            name, c0, c1, bb = self.pc_jobs.pop(0)
            self.dma("gpsimd", self.wb[name][:, c0:c1], self.wf[name][:, c0:c1], list(after), [bb])

    def wdep(self, name, c0=0, c1=1 << 30):
        return [bb for (a0, a1, bb) in self.wblocks[name] if a0 < c1 and c0 < a1]

    def alloc_core(self):
        sb = self.sb
        for i in range(6):
            t = self.es.enter_context(self.nc.psum_tensor("bank%d" % i, [128, 512], F32))
            self.banks.append(T(t, "bank%d" % i, excl=True))
        for i in range(2):
            t = self.es.enter_context(self.nc.psum_tensor("tbank%d" % i, [128, 1024], BF16))
            self.tbanks.append(T(t, "tbank%d" % i, excl=True))
        self.identf = sb("identf", [128, 128], F32)
        self.identb = sb("identb", [128, 128], BF16)
        self.trif = sb("trif", [128, 128], F32)
        self.onesf = sb("onesf", [128, 128], F32)
        self.onesb = sb("onesb", [128, 128], BF16)
        self.consts = sb("consts", [128, 8], F32)
        self.Mw = sb("Mw", [128, 32, 128], BF16)
        self.WsRe = sb("WsRe", [128, 32, 64], BF16)
        self.WsIm = sb("WsIm", [128, 32, 64], BF16)
        self.VwRe = sb("VwRe", [128, 16, 128], BF16)
        self.VwIm = sb("VwIm", [128, 16, 128], BF16)
        self.cpr = sb("cpr", [128, 7, 16], F32)
        self.cpi = sb("cpi", [128, 7, 16], F32)

    def set_par(self, p):
        self.xt, self.hT, self.ss, self.rstd = self.xtb[p], self.hTb[p], self.ssb[p], self.rstdb[p]

    def set_par2(self, q):
        a = self.alt[q]
        self.Uall, self.vext, self.wk2, self.ebg, self.kbase = a["Uall"], a["vext"], a["wk2"], a["ebg"], a["kbase"]

    def alloc_work(self):
        sb = self.sb
        for i in range(9):
            self.tmpf.append(sb("tmpf%d" % i, [128, 516], F32))
        for i in range(3):
            self.tmpb.append(sb("tmpb%d" % i, [128, 1024], BF16))
        for i in range(4):
            self.wslots.append(sb("wslot%d" % i, [128, 4096], BF16))
        self.wpool = {"front": [0, 1], "back": [2, 3]}
        self.wpool_i = {"front": 0, "back": 0}
        self.wphase = "front"
        self.g1T = sb("g1T", [128, 8], F32)
        self.g2T = sb("g2T", [128, 8], F32)
        self.g3T = sb("g3T", [128, 8], F32)
        self.gmT = sb("gmT", [128, 8], F32)
        self.cw = sb("cw", [128, 8, 4], F32)
        self.cb = sb("cb", [128, 8], F32)
        self.bif = sb("bif", [128, 8], F32)
        self.mng = sb("mng", [128, 512], F32)
        self.bgT = sb("bgT", [128, 16], F32)
        self.bgluT = sb("bgluT", [128, 4], F32)
        self.fcw = sb("fcw", [128, 44, 3], F32)
        self.fcb = sb("fcb", [128, 44], F32)
        self.vmask = sb("vmask_s", [128, NCHT], F32)
        self.hvalid = sb("hvalid_s", [128, 1], F32)
        self.ifslab = sb("ifslab", [128, 8, 8], BF16)
        self.carry = sb("carry", [128, 16, 2], F32)
        self.Cst = sb("Cst", [128, 4, 129], F32)
        self.Cbf = sb("Cbf", [128, 4, 129], BF16)
        self.qkh = sb("qkh", [128, 8, 3], F32).split([0, 1])
        self.uprev = sb("uprev", [128, 44, 2], F32).split([0, 1])
        self.kxT = sb("kxT", [128, 8, 256], BF16)
        self.vx = sb("vx", [128, 2, 1024], BF16)
        self.xtb = [sb("xt%d" % i, [128, 4, 1024], F32).split([(c, h) for c in range(4) for h in range(2)]) for i in range(2)]
        self.ssb = [sb("ss%d" % i, [128, 4], F32) for i in range(2)]
        self.rstdb = [sb("rstd%d" % i, [128, 4], F32) for i in range(2)]
        self.hTb = [sb("hT%d" % i, [128, 8, 512], BF16) for i in range(2)]
        self.set_par(0)
        self.usT = sb("usT", [128, 4, 8, 64], BF16)
        self.Uall = sb("Uall", [128, 32, 64], BF16)
        self.X0 = sb("X0", [128, 16, 2, 64], F32).split([(h, e) for h in range(2) for e in range(2)])
        self.Xp = sb("Xp", [128, 16, 2, 64], BF16)
        self.stA = sb("stA", [128, 1024], F32).split(["re", "im"])
        self.stB = sb("stB", [128, 1024], F32).split(["re", "im"])
        self.ys5T = sb("ys5T", [128, 4, 512], BF16)
        self.qkT = sb("qkT", [128, 8, 512], BF16)
        self.vext = sb("vext", [128, 4, 4, 129], BF16)
        self.Ygl = TV(self.vext, self.vext[:, :, :, :].rearrange("p a b c -> p (a b c)")[:, 0:2048].rearrange("p (g j) -> p g j", g=32))
        self.sgo = TV(self.ys5T, self.ys5T[:, :, :])
        self.gif = sb("gif", [128, 4, 8], F32)
        self.l1 = sb("l1", [128, 4, 4], F32)
        self.l1b = sb("l1b", [128, 4, 4], F32)
        self.nbg = sb("nbg", [128, 4, 8], F32)
        self.wk = sb("wk", [128, 4, 4], F32)
        self.wk2 = sb("wk2", [128, 4, 4], F32)
        self.sfx = sb("sfx", [128, 4, 4], F32)
        self.wkf = sb("wkf", [128, 4, 4], F32)
        self.ebg = sb("ebg", [128, 4, 8], F32)
        self.sm = sb("sm", [128, 16], F32)
        self.hss = sb("hss", [128, 12], F32)
        self.junk2 = sb("junk2", [128, 128], BF16)
        self.ymT = TV(self.usT, self.usT[:, :, :, :].rearrange("p a s j -> p a (s j)"))
        self.mergedT = sb("mergedT", [128, 8, 512], BF16)
        self.gT = sb("gT", [128, 6, 512], BF16)
        Uall_alt = TV(self.gT, self.gT[:, :, :].rearrange("p a b -> p (a b)")[:, 0:2048].rearrange("p (g j) -> p g j", g=32))
        vext_alt = TV(self.mergedT, self.mergedT[:, :, :].rearrange("p a b -> p (a b)")[:, 0:2064].rearrange("p (c h e) -> p c h e", c=4, h=4))
        self.alt = [
            {"Uall": self.Uall, "vext": self.vext, "wk2": self.wk2, "ebg": self.ebg, "kbase": 4},
            {"Uall": Uall_alt, "vext": vext_alt, "wk2": sb("wk2b", [128, 4, 4], F32), "ebg": sb("ebgb", [128, 4, 8], F32), "kbase": 0},
        ]
        self.kbase = 4
        self.hg = sb("hg", [128, 512], F32)

    def load_consts(self):
        d = self.din
        dm = self.dma
        dm("gpsimd", self.identb[:], d["c_ident"][:, :], [], [self.identb])
        dm("sync", self.trif[:], d["c_tri"][:, :], [], [self.trif])
        dm("sync", self.onesf[:], d["c_ones"][:, :], [], [self.onesf])
        dm("gpsimd", self.onesb[:], d["c_ones"][:, :], [], [self.onesb])
        for t, nm in ((self.g1T, "mix_norm_g"), (self.g2T, "x_norm_g"), (self.g3T, "f_norm_g"), (self.gmT, "mem_norm_g")):
            dm("sync", t[:], d[nm].t.rearrange("(k p) -> p k", p=128), [], [t], allow_slow_non_contiguous=True)
        for k in range(4):
            dm("sync", self.cw[:, :, k], d["m_conv_w"].t[k, :].rearrange("(m p) -> p m", p=128), [], [self.cw], allow_slow_non_contiguous=True)
        dm("sync", self.cb[:], d["m_conv_b"].t.rearrange("(m p) -> p m", p=128), [], [self.cb], allow_slow_non_contiguous=True)
        dm("sync", self.bif[:, 0:4], d["m_b_i"].t.partition_broadcast(128), [], [self.bif])
        dm("sync", self.bif[:, 4:8], d["m_b_f"].t.partition_broadcast(128), [], [self.bif])
        dm("sync", self.mng[:], d["m_norm_g"].t.partition_broadcast(128), [], [self.mng])
        dm("sync", self.bgT[:], d["b_gate"].t.rearrange("(m p) -> p m", p=128), [], [self.bgT], allow_slow_non_contiguous=True)
        dm("sync", self.bgluT[:], d["s5_b_glu"].t.rearrange("(m p) -> p m", p=128), [], [self.bgluT], allow_slow_non_contiguous=True)
        for k in range(3):
            dm("sync", self.fcw[:, :, k], d["f_conv_w"].t[k, :].rearrange("(m p) -> p m", p=128), [], [self.fcw], allow_slow_non_contiguous=True)
        dm("sync", self.fcb[:], d["f_conv_b"].t.rearrange("(m p) -> p m", p=128), [], [self.fcb], allow_slow_non_contiguous=True)
        dm("sync", self.vmask[:], d["vmask"][:, :], [], [self.vmask])
        dm("sync", self.hvalid[:], d["hvalid"][:, :], [], [self.hvalid])
        for t_, ap_ in ((self.cw, self.cw[:]), (self.cb, self.cb[:]), (self.fcw, self.fcw[:, 0:22, :]),
                        (self.fcb, self.fcb[:, 0:22]), (self.bgT, self.bgT[:]), (self.bgluT, self.bgluT[:])):
            self.act(ap_, ap_, AF.Copy, [t_], [t_], scale=0.5)
        for t in (self.carry, self.Cst, self.Cbf, self.qkh, self.uprev):
            self.memset("gpsimd", t[:], 0.0, [t])
        self.memset("gpsimd", self.vext[:], 1.0, [self.vext])
        self.memset("gpsimd", self.alt[1]["vext"][:, :, :, :], 1.0, [self.alt[1]["vext"]])

    def s5_setup(self):
        with contextlib.ExitStack() as es2:
            sb = lambda n, s, dt=F32: self.sb("s5s_" + n, s, dt, es2)
            d = self.din
            V = self.V
            lamr, lami = sb("lamr", [128, 32]), sb("lami", [128, 32])
            dtb = sb("dtb", [128, 32])
            Br, Bi = sb("Br", [128, 32, 16]), sb("Bi", [128, 32, 16])
            cin_r, cin_i = sb("cinr", [128, 4, 2, 64]), sb("cini", [128, 4, 2, 64])
            Cr, Ci = sb("Cr", [128, 32, 16]), sb("Ci", [128, 32, 16])
            dcol = sb("dcol", [128, 32])
            mmask = sb("mmask", [128, 128])
            for h0 in (0, 64):
                self.dma("sync", lamr[h0:h0 + 64, :], d["s5_lam_re"].t.rearrange("g p -> p g"), [], [lamr], allow_slow_non_contiguous=True)
                self.dma("sync", lami[h0:h0 + 64, :], d["s5_lam_im"].t.rearrange("g p -> p g"), [], [lami], allow_slow_non_contiguous=True)
                self.dma("sync", Br[h0:h0 + 64, :, :], d["s5_b_re"].t.rearrange("g p h -> p g h"), [], [Br])
                self.dma("sync", Bi[h0:h0 + 64, :, :], d["s5_b_im"].t.rearrange("g p h -> p g h"), [], [Bi])
            self.dma("sync", dtb[:], d["s5_log_dt"].t.partition_broadcast(128), [], [dtb])
            for e in (0, 1):
                self.dma("sync", cin_r[:, :, e, :], d["s5_c_re"].t.rearrange("(a q) p -> q a p", q=128), [], [cin_r])
                self.dma("sync", cin_i[:, :, e, :], d["s5_c_im"].t.rearrange("(a q) p -> q a p", q=128), [], [cin_i])
            for s in range(8):
                self.dma("sync", dcol[s * 16:(s + 1) * 16, :], d["s5_d"].t.rearrange("(g i) -> i g", i=16), [], [dcol], allow_slow_non_contiguous=True)
            self.dma("sync", mmask[:], d["c_mmask"][:, :], [], [mmask])
            for src, dst in ((cin_r, Cr), (cin_i, Ci)):
                for a in range(4):
                    pb = self.bank()
                    self.tr(pb[:, 0:128], src[:, a, :, :], self.identf[:], [src, self.identf], [pb])
                    self.cp("vector", dst[:, a * 8:(a + 1) * 8, :], pb[:, 0:128].rearrange("p (g o) -> p g o", o=16), [pb], [dst])
            self.act(dtb[:], dtb[:], AF.Exp, [dtb], [dtb])
            rho, th = sb("rho", [128, 32]), sb("th", [128, 32])
            self.tt("vector", rho[:], lamr[:], dtb[:], ALU.mult, [lamr, dtb], [rho])
            self.act(rho[:], rho[:], AF.Exp, [rho], [rho])
            self.tt("vector", th[:], lami[:], dtb[:], ALU.mult, [lami, dtb], [th])
            ki, kf = sb("ki", [128, 32], I32), sb("kf", [128, 32])
            self.ts("vector", kf[:], th[:], 1.0 / (2 * math.pi), None, ALU.mult, None, [th], [kf])
            self.cp("vector", ki[:], kf[:], [kf], [ki])
            self.cp("vector", kf[:], ki[:], [ki], [kf])
            self.stt(th[:], kf[:], -2 * math.pi, th[:], ALU.mult, ALU.add, [kf, th], [th])
            s4, c4 = sb("s4", [128, 32]), sb("c4", [128, 32])
            self.act(s4[:], th[:], AF.Sin, [th], [s4], scale=0.25)
            self.act(c4[:], th[:], AF.Sin, [th, self.consts], [c4], scale=0.25, bias=self.consts[:, 1:2])
            tA, tB, tC = sb("tA", [128, 32]), sb("tB", [128, 32]), sb("tC", [128, 32])

            def csq(cr, ci):
                self.tt("vector", tA[:], cr[:], cr[:], ALU.mult, [cr], [tA])
                self.tt("vector", tB[:], ci[:], ci[:], ALU.mult, [ci], [tB])
                self.tt("vector", tC[:], cr[:], ci[:], ALU.mult, [cr, ci], [tC])
                self.tt("vector", cr[:], tA[:], tB[:], ALU.subtract, [tA, tB], [cr])
                self.ts("vector", ci[:], tC[:], 2.0, None, ALU.mult, None, [tC], [ci])
            csq(c4, s4)
            csq(c4, s4)
            ar, ai = sb("ar", [128, 32]), sb("ai", [128, 32])
            self.tt("vector", ar[:], rho[:], c4[:], ALU.mult, [rho, c4], [ar])
            self.tt("vector", ai[:], rho[:], s4[:], ALU.mult, [rho, s4], [ai])

            def cmul(orr, oi, xr, xi, yr, yi, shp=None):
                (orr_a, orr_t), (oi_a, oi_t) = orr, oi
                (xr_a, xr_t), (xi_a, xi_t), (yr_a, yr_t), (yi_a, yi_t) = xr, xi, yr, yi
                u1, u2 = sb_tmp(shp)
                self.tt("vector", u1[0], xr_a, yr_a, ALU.mult, [xr_t, yr_t], [u1[1]])
                self.tt("vector", u2[0], xi_a, yi_a, ALU.mult, [xi_t, yi_t], [u2[1]])
                u3, u4 = sb_tmp(shp)
                self.tt("vector", u3[0], xr_a, yi_a, ALU.mult, [xr_t, yi_t], [u3[1]])
                self.tt("vector", u4[0], xi_a, yr_a, ALU.mult, [xi_t, yr_t], [u4[1]])
                self.tt("vector", orr_a, u1[0], u2[0], ALU.subtract, [u1[1], u2[1]], [orr_t])
                self.tt("vector", oi_a, u3[0], u4[0], ALU.add, [u3[1], u4[1]], [oi_t])

            tmp_pool = [sb("cm%d" % i, [128, 8 * 8 * 16]) for i in range(8)]
            tmp_i = [0]

            def sb_tmp(shp):
                res = []
                for _ in range(2):
                    t = tmp_pool[tmp_i[0] % 8]
                    tmp_i[0] += 1
                    n = int(np.prod(shp[1:]))
                    ap = t[:, 0:n]
                    if len(shp) == 3:
                        ap = ap.rearrange("p (a b) -> p a b", a=shp[1])
                    elif len(shp) == 4:
                        ap = ap.rearrange("p (a b c) -> p a b c", a=shp[1], b=shp[2])
                    res.append((ap, t))
                return res

            apr, api = sb("apr", [128, 9, 32]), sb("api", [128, 9, 32])
            self.memset("vector", apr[:, 0, :], 1.0, [apr])
            self.memset("vector", api[:, 0, :], 0.0, [api])
            self.cp("vector", apr[:, 1, :], ar[:], [ar], [apr])
            self.cp("vector", api[:, 1, :], ai[:], [ai], [api])
            for k in range(2, 9):
                cmul((apr[:, k, :], apr), (api[:, k, :], api), (apr[:, k - 1, :], apr), (api[:, k - 1, :], api),
                     (ar[:], ar), (ai[:], ai), [128, 32])
            n2, ivr, ivi = sb("n2", [128, 32]), sb("ivr", [128, 32]), sb("ivi", [128, 32])
            self.tt("vector", tA[:], ar[:], ar[:], ALU.mult, [ar], [tA])
            self.tt("vector", tB[:], ai[:], ai[:], ALU.mult, [ai], [tB])
            self.tt("vector", n2[:], tA[:], tB[:], ALU.add, [tA, tB], [n2])
            self.V(lambda E: E.reciprocal(out=n2[:], in_=n2[:]), [n2], [n2])
            self.tt("vector", ivr[:], ar[:], n2[:], ALU.mult, [ar, n2], [ivr])
            self.stt(ivi[:], ai[:], -1.0, n2[:], ALU.mult, ALU.mult, [ai, n2], [ivi])
            ipr, ipi = sb("ipr", [128, 8, 32]), sb("ipi", [128, 8, 32])
            self.memset("vector", ipr[:, 0, :], 1.0, [ipr])
            self.memset("vector", ipi[:, 0, :], 0.0, [ipi])
            self.cp("vector", ipr[:, 1, :], ivr[:], [ivr], [ipr])
            self.cp("vector", ipi[:, 1, :], ivi[:], [ivi], [ipi])
            for k in range(2, 8):
                cmul((ipr[:, k, :], ipr), (ipi[:, k, :], ipi), (ipr[:, k - 1, :], ipr), (ipi[:, k - 1, :], ipi),
                     (ivr[:], ivr), (ivi[:], ivi), [128, 32])
            for nm_, t_ in (("ar", ar), ("ai", ai), ("ivr", ivr), ("ivi", ivi)):
                self.dump(nm_, t_, t_[:, :])
            for nm_, t_ in (("ipr", ipr), ("ipi", ipi), ("apr", apr), ("api", api)):
                self.dump(nm_, t_, t_[:, :, :])
            dr, di = sb("dr", [128, 7, 32]), sb("di", [128, 7, 32])
            self.cp("vector", dr[:, 0, :], apr[:, 8, :], [apr], [dr])
            self.cp("vector", di[:, 0, :], api[:, 8, :], [api], [di])
            for k in range(1, 7):
                cmul((dr[:, k, :], dr), (di[:, k, :], di), (dr[:, k - 1, :], dr), (di[:, k - 1, :], di),
                     (dr[:, k - 1, :], dr), (di[:, k - 1, :], di), [128, 32])
            for (src, dst) in ((dr, self.cpr), (di, self.cpi)):
                v = src[:, :, :].rearrange("p k (a e) -> p k a e", e=2)
                self.cp("vector", dst[0:64, :, :], v[0:64, :, :, 0], [src], [dst])
                self.cp("vector", dst[64:128, :, :], v[64:128, :, :, 1], [src], [dst])
            qr, qi = sb("qr", [128, 32]), sb("qi", [128, 32])
            am1 = sb("am1", [128, 32])
            self.ts("vector", am1[:], ar[:], -1.0, None, ALU.add, None, [ar], [am1])
            self.tt("vector", tA[:], lamr[:], lamr[:], ALU.mult, [lamr], [tA])
            self.tt("vector", tB[:], lami[:], lami[:], ALU.mult, [lami], [tB])
            self.tt("vector", n2[:], tA[:], tB[:], ALU.add, [tA, tB], [n2])
            self.V(lambda E: E.reciprocal(out=n2[:], in_=n2[:]), [n2], [n2])
            self.tt("vector", tA[:], am1[:], lamr[:], ALU.mult, [am1, lamr], [tA])
            self.tt("vector", tB[:], ai[:], lami[:], ALU.mult, [ai, lami], [tB])
            self.tt("vector", qr[:], tA[:], tB[:], ALU.add, [tA, tB], [qr])
            self.tt("vector", tA[:], ai[:], lamr[:], ALU.mult, [ai, lamr], [tA])
            self.tt("vector", tB[:], am1[:], lami[:], ALU.mult, [am1, lami], [tB])
            self.tt("vector", qi[:], tA[:], tB[:], ALU.subtract, [tA, tB], [qi])
            self.tt("vector", qr[:], qr[:], n2[:], ALU.mult, [qr, n2], [qr])
            self.tt("vector", qi[:], qi[:], n2[:], ALU.mult, [qi, n2], [qi])
            Bbr, Bbi = sb("Bbr", [128, 32, 16]), sb("Bbi", [128, 32, 16])
            bc = lambda t: t[:, :].unsqueeze(2).to_broadcast([128, 32, 16])
            cmul((Bbr[:], Bbr), (Bbi[:], Bbi), (bc(qr), qr), (bc(qi), qi), (Br[:], Br), (Bi[:], Bi), [128, 32, 16])

            Wr, Wi = sb("Wr", [128, 8, 8, 16]), sb("Wi", [128, 8, 8, 16])
            Vr, Vi = sb("Vr", [128, 8, 8, 16]), sb("Vi", [128, 8, 8, 16])
            Wstk, Vstk = sb("Wstk", [128, 8, 128]), sb("Vstk", [128, 8, 128])
            mtmp = sb("mtmp", [128, 128])

            def pw_b(src, k, gb):
                return src[:, k, gb * 8:(gb + 1) * 8]

            aRr, aRi = sb("aRr", [128, 8, 32]), sb("aRi", [128, 8, 32])
            for s in range(8):
                self.cp("vector", aRr[:, s, :], apr[:, 7 - s, :], [apr], [aRr])
                self.cp("vector", aRi[:, s, :], api[:, 7 - s, :], [api], [aRi])

            def pw4(t, k0, gs):
                return t[:, k0:k0 + 8, gs].rearrange("p k g -> p g k").unsqueeze(3).to_broadcast([128, 8, 8, 16])

            def b4(t, gs):
                return t[:, gs, :].unsqueeze(2).to_broadcast([128, 8, 8, 16])

            for gb in range(4):
                gs = slice(gb * 8, (gb + 1) * 8)
                cmul((Wr[:], Wr), (Wi[:], Wi), (pw4(ipr, 0, gs), ipr), (pw4(ipi, 0, gs), ipi),
                     (b4(Bbr, gs), Bbr), (b4(Bbi, gs), Bbi), [128, 8, 8, 16])
                cmul((Vr[:], Vr), (Vi[:], Vi), (pw4(apr, 0, gs), apr), (pw4(api, 0, gs), api),
                     (b4(Cr, gs), Cr), (b4(Ci, gs), Ci), [128, 8, 8, 16])
                fl = lambda t: t[:, :, :, :].rearrange("p g s i -> p g (s i)")
                self.cp("vector", Wstk[0:64, :, :], fl(Wr)[0:64], [Wr], [Wstk])
                self.cp("vector", Wstk[64:128, :, :], fl(Wi)[64:128], [Wi], [Wstk])
                self.cp("vector", Vstk[0:64, :, :], fl(Vr)[0:64], [Vr], [Vstk])
                self.ts("vector", Vstk[64:128, :, :], fl(Vi)[64:128], -1.0, None, ALU.mult, None, [Vi], [Vstk])
                for gl in range(8):
                    g = gb * 8 + gl
                    pb = self.bank()
                    self.mm(pb[:, 0:128], Wstk[:, gl, :], Vstk[:, gl, :], True, True, [Wstk, Vstk], [pb])
                    self.tt("vector", mtmp[:], pb[:, 0:128], mmask[:], ALU.mult, [pb, mmask], [mtmp])
                    self.stt(self.Mw[:, g, :], self.identf[:], dcol[:, g:g + 1], mtmp[:], ALU.mult, ALU.add,
                             [self.identf, dcol, mtmp], [self.Mw])
                cmul((Wr[:], Wr), (Wi[:], Wi), (pw4(aRr, 0, gs), aRr), (pw4(aRi, 0, gs), aRi),
                     (b4(Bbr, gs), Bbr), (b4(Bbi, gs), Bbi), [128, 8, 8, 16])
                for (src, dst) in ((Wr, self.WsRe), (Wi, self.WsIm)):
                    for gl in range(8):
                        g = gb * 8 + gl
                        pb = self.bank()
                        self.tr(pb[:, 0:64], fl(src)[0:64, gl, :], self.identf[0:64, 0:64], [src, self.identf], [pb])
                        self.cp("vector", dst[:, g, :], pb[:, 0:64], [pb], [dst])
                cmul((Vr[:], Vr), (Vi[:], Vi), (pw4(apr, 1, gs), apr), (pw4(api, 1, gs), api),
                     (b4(Cr, gs), Cr), (b4(Ci, gs), Ci), [128, 8, 8, 16])
                vre = fl(Vr).rearrange("p (a e) f -> p a e f", e=2)
                vim = fl(Vi).rearrange("p (a e) f -> p a e f", e=2)
                ps_ = slice(gb * 4, (gb + 1) * 4)
                self.cp("vector", self.VwRe[0:64, ps_, :], vre[0:64, :, 0, :], [Vr], [self.VwRe])
                self.cp("vector", self.VwRe[64:128, ps_, :], vre[64:128, :, 1, :], [Vr], [self.VwRe])
                self.ts("vector", self.VwIm[0:64, ps_, :], vim[0:64, :, 0, :], -1.0, None, ALU.mult, None, [Vi], [self.VwIm])
                self.ts("vector", self.VwIm[64:128, ps_, :], vim[64:128, :, 1, :], -1.0, None, ALU.mult, None, [Vi], [self.VwIm])
            self.mk.barrier()
            self.mk.emit()

    def w3(self, name, c0, c1):
        return self.wb[name].t.rearrange("(k p) n -> p k n", p=128)[:, :, c0:c1]

    def norm_T(self, nch, gT):
        xt, ss, rstd, hT = self.xt, self.ss, self.rstd, self.hT
        self.memset("vector", ss[:], 0.0, [ss])
        for c in range(nch):
            junk = self.tb()
            self.act(junk[:], xt[:, c, :], AF.Square, [xt.s(c)], [junk, ss], accum_out=ss[:, c:c + 1])
        self.ts("vector", rstd[:, 0:nch], ss[:, 0:nch], 1.0 / D, 1e-6, ALU.mult, ALU.add, [ss], [rstd])
        self.tt("gpsimd", rstd[:, 0:nch], rstd[:, 0:nch], self.consts[:, 2:3].to_broadcast([128, nch]), ALU.pow,
                [rstd, self.consts], [rstd])
        for c in range(nch):
            xn = self.tb()
            self.act(xn[:], xt[:, c, :], AF.Copy, [xt.s(c), rstd], [xn], scale=rstd[:, c:c + 1])
            tbk = self.tbank()
            for k in range(8):
                self.tr(tbk[:, k * 128:(k + 1) * 128], xn[:, k * 128:(k + 1) * 128], self.identb[:], [xn, self.identb], [tbk])
            self.tt("vector", hT[:, :, c * 128:(c + 1) * 128], tbk[:, :].rearrange("p (k t) -> p k t", k=8),
                    gT[:, :].unsqueeze(2).to_broadcast([128, 8, 128]), ALU.mult, [tbk, gT], [hT])

    def fm_matmul(self, pb, wv, c0, src, kt, N, rd):
        for k in range(kt):
            self.mm(pb[:, 0:N], wv[:, k, c0:c0 + 128], src[:, k, 0:N], k == 0, k == kt - 1, rd, [pb])

    def tm_matmul(self, pb, src, c, wv, c0, ncols, kt, rd):
        for k in range(kt):
            self.mm(pb[:, 0:ncols], src[:, k, c * 128:(c + 1) * 128], wv[:, k, c0:c0 + ncols], k == 0, k == kt - 1, rd, [pb])

    def conv_silu2(self, items, N):
        tmp = [(self.tf(), self.tf()) for _ in items]
        for (pb, mt, dst), (raw, acc) in zip(items, tmp):
            hs = self.qkh.s(mt % 2)
            self.cp("scalar", raw[:, 0:3], self.qkh[:, mt, :], [hs], [raw])
            self.cp("scalar", raw[:, 3:3 + N], pb[:, 0:N], [pb], [raw])
            self.act(acc[:, 0:N], pb[:, 0:N], AF.Identity, [pb, self.cw, self.cb], [acc],
                     scale=self.cw[:, mt, 3:4], bias=self.cb[:, mt:mt + 1])
            self.cp("scalar", self.qkh[:, mt, :], raw[:, N:N + 3], [raw], [hs])
        for k in (2, 1, 0):
            for (pb, mt, dst), (raw, acc) in zip(items, tmp):
                self.stt(acc[:, 0:N], raw[:, k:k + N], self.cw[:, mt, k:k + 1], acc[:, 0:N], ALU.mult, ALU.add,
                         [raw, self.cw, acc], [acc])
        for (pb, mt, dst), (raw, acc) in zip(items, tmp):
            self.act(raw[:, 0:N], acc[:, 0:N], AF.Tanh, [acc], [raw])
        for (pb, mt, dst), (raw, acc) in zip(items, tmp):
            self.stt(self.qkT[:, dst, 0:N], raw[:, 0:N], 1.0, acc[:, 0:N], ALU.add, ALU.mult, [raw, acc], [self.qkT])

    def gates_all(self, nch, cg0, fold=False):
        gif, l1, nbg, wk, ebg = self.gif, self.l1, self.nbg, self.wk, self.ebg
        pb = self.bank()
        for c in range(nch):
            for k in range(8):
                self.mm(pb[:, c * 8:(c + 1) * 8], self.hT[:, k, c * 128:(c + 1) * 128], self.ifslab[:, k, 0:8], k == 0, k == 7,
                        [self.hT, self.ifslab], [pb])
        self.tt("vector", gif[:, 0:nch, :], pb[:, 0:nch * 8].rearrange("p (c e) -> p c e", e=8),
                self.bif[:, :].unsqueeze(1).to_broadcast([128, nch, 8]), ALU.add, [pb, self.bif], [gif])
        self.act(l1[:, 0:nch, :], gif[:, 0:nch, 4:8], AF.Exp, [gif], [l1], scale=-1.0)
        self.act(self.l1b[:, 0:nch, :], l1[:, 0:nch, :], AF.Ln, [l1, self.consts], [self.l1b], bias=self.consts[:, 0:1])
        pb2 = self.bank()
        for c in range(nch):
            self.mm(pb2[:, c * 8:c * 8 + 4], self.trif[:], self.l1b[:, c, :], True, True, [self.trif, self.l1b], [pb2])
            self.mm(pb2[:, c * 8 + 4:c * 8 + 8], self.onesf[:], self.l1b[:, c, :], True, True, [self.onesf, self.l1b], [pb2])
        self.cp("vector", nbg[:, 0:nch, :], pb2[:, 0:nch * 8].rearrange("p (c e) -> p c e", e=8), [pb2], [nbg])
        self.tt("vector", wk[:, 0:nch, :], gif[:, 0:nch, 0:4], nbg[:, 0:nch, 0:4], ALU.add, [gif, nbg], [wk])
        if fold:
            sfx = self.sfx
            self.cp("vector", sfx[:, nch - 1, :], nbg[:, nch - 1, 4:8], [nbg], [sfx])
            for c in range(nch - 2, -1, -1):
                self.tt("vector", sfx[:, c, :], nbg[:, c, 4:8], sfx[:, c + 1, :], ALU.add, [nbg, sfx], [sfx])
            self.tt("vector", self.wkf[:, 0:nch, :], wk[:, 0:nch, :], sfx[:, 0:nch, :], ALU.subtract, [wk, sfx], [self.wkf])
            self.act(l1[:, 0:nch, :], self.wkf[:, 0:nch, :], AF.Exp, [self.wkf], [l1])
            self.stt(self.wk2[:, 0:nch, :], l1[:, 0:nch, :], 128.0 ** -0.5,
                     self.vmask[:, cg0:cg0 + nch].unsqueeze(2).to_broadcast([128, nch, 4]), ALU.mult, ALU.mult,
                     [l1, self.vmask], [self.wk2])
            self.act(ebg[:, 0, 0:4], sfx[:, 0, :], AF.Exp, [sfx], [ebg], scale=-1.0)
            return
        self.act(l1[:, 0:nch, :], wk[:, 0:nch, :], AF.Exp, [wk], [l1])
        self.stt(self.wk2[:, 0:nch, :], l1[:, 0:nch, :], 128.0 ** -0.5,
                 self.vmask[:, cg0:cg0 + nch].unsqueeze(2).to_broadcast([128, nch, 4]), ALU.mult, ALU.mult,
                 [l1, self.vmask], [self.wk2])
        self.act(ebg[:, 0:nch, :], nbg[:, 0:nch, :], AF.Exp, [nbg], [ebg], scale=-1.0)

    def mlstm_prefix(self, nch):
        hb = [self.bank() for _ in range(4)]
        for c in range(nch):
            cs = slice(c * 128, (c + 1) * 128)
            wb4 = self.wk2[:, c, :].unsqueeze(2).to_broadcast([128, 4, 128])
            tbk = self.tbank()
            for hh in range(4):
                self.tr(tbk[:, hh * 128:(hh + 1) * 128], self.qkT[:, self.kbase + hh, cs], self.identb[:], [self.qkT, self.identb], [tbk])
            ktok = self.tb()
            kt4 = ktok[:, 0:512].rearrange("p (h d) -> p h d", h=4)
            self.tt("vector", kt4, tbk[:, 0:512].rearrange("p (h d) -> p h d", h=4), wb4, ALU.mult, [tbk, self.wk2], [ktok])
            for hh in range(4):
                self.mm(hb[hh][:, 0:129], kt4[:, hh, :], self.vext[:, c, hh, :], c == 0, c == nch - 1, [ktok, self.vext], [hb[hh]])
        ct = self.stB
        ct4 = ct[:, 0:516].rearrange("p (h e) -> p h e", h=4)
        self.tt("vector", ct4, self.Cst[:, :, :], self.ebg[:, 0, 0:4].unsqueeze(2).to_broadcast([128, 4, 129]), ALU.mult,
                [self.Cst, self.ebg], [ct])
        for hh in range(4):
            self.tt("vector", self.Cst[:, hh, :], hb[hh][:, 0:129], ct4[:, hh, :], ALU.add, [hb[hh], ct], [self.Cst])
        self.cp("scalar", self.Cbf[:, :, :], self.Cst[:, :, :], [self.Cst], [self.Cbf])
        yield

    def vext_cur_ap(self, c, hh):
        return self.vext[:, c, hh, :]

    def mlstm_chunk(self, c, full):
        cs = slice(c * 128, (c + 1) * 128)
        wb4 = self.wk2[:, c, :].unsqueeze(2).to_broadcast([128, 4, 128])
        tbk = self.tbank()
        for hh in range(4):
            self.tr(tbk[:, hh * 128:(hh + 1) * 128], self.qkT[:, self.kbase + hh, cs], self.identb[:], [self.qkT, self.identb], [tbk])
        if full:
            pbA = self.bank()
            for hh in range(4):
                self.mm(pbA[:, hh * 128:(hh + 1) * 128], self.qkT[:, 4 + hh, cs], self.qkT[:, hh, cs], True, True, [self.qkT], [pbA])
        ktok = self.tb()
        kt4 = ktok[:, 0:512].rearrange("p (h d) -> p h d", h=4)
        self.tt("vector", kt4, tbk[:, 0:512].rearrange("p (h d) -> p h d", h=4), wb4, ALU.mult, [tbk, self.wk2], [ktok])
        if full:
            at = self.tb()
            at4 = at[:, 0:512].rearrange("p (h d) -> p h d", h=4)
            self.tt("vector", at4, pbA[:, 0:512].rearrange("p (h d) -> p h d", h=4), wb4, ALU.mult, [pbA, self.wk2], [at])
            self.tt("vector", at4, at4, self.trif[:, :].unsqueeze(1).to_broadcast([128, 4, 128]), ALU.mult, [at, self.trif], [at])
        bP, bQ, bR = self.bank(), self.bank(), self.bank()
        numloc = [(bP, 0), (bP, 129), (bP, 258), (bQ, 0)]
        dcloc = [(bQ, 129), (bQ, 258), (bR, 0), (bR, 129)]
        if full:
            for hh in range(4):
                bk, o = numloc[hh]
                self.mm(bk[:, o:o + 129], at4[:, hh, :], self.vext[:, c, hh, :], True, False, [at, self.vext], [bk])
                self.mm(bk[:, o:o + 129], self.qkT[:, hh, cs], self.Cbf[:, hh, :], False, True, [self.qkT, self.Cbf], [bk])
        for hh in range(4):
            bk, o = dcloc[hh]
            self.mm(bk[:, o:o + 129], kt4[:, hh, :], self.vext[:, c, hh, :], True, True, [ktok, self.vext], [bk])
        if full:
            sm = self.sm
            for hh in range(4):
                bk, o = numloc[hh]
                self.act(sm[:, hh:hh + 1], bk[:, o + 128:o + 129], AF.Abs, [bk, self.ebg], [sm], scale=self.ebg[:, c, hh:hh + 1])
            self.ts("vector", sm[:, 4:8], sm[:, 0:4], 1.0, None, ALU.max, None, [sm], [sm])
            self.V(lambda E: E.reciprocal(out=sm[:, 8:12], in_=sm[:, 4:8]), [sm], [sm])
            self.tt("vector", sm[:, 12:16], sm[:, 8:12], self.ebg[:, c, 0:4], ALU.mult, [sm, self.ebg], [sm])
            for hh in range(4):
                bk, o = numloc[hh]
                self.stt(self.hg[:, hh * 128:(hh + 1) * 128], bk[:, o:o + 128], sm[:, 12 + hh:13 + hh],
                         self.sgo[:, c, hh * 128:(hh + 1) * 128], ALU.mult, ALU.mult, [bk, sm, self.sgo], [self.hg])
        ct = self.stB
        ct4 = ct[:, 0:516].rearrange("p (h e) -> p h e", h=4)
        self.tt("vector", ct4[:, 0:2, :], bQ[:, 129:387].rearrange("p (h e) -> p h e", h=2), self.Cst[:, 0:2, :], ALU.add,
                [bQ, self.Cst], [ct])
        self.tt("vector", ct4[:, 2:4, :], bR[:, 0:258].rearrange("p (h e) -> p h e", h=2), self.Cst[:, 2:4, :], ALU.add,
                [bR, self.Cst], [ct])
        self.tt("vector", self.Cst[:, :, :], ct4, self.ebg[:, c, 4:8].unsqueeze(2).to_broadcast([128, 4, 129]), ALU.mult,
                [ct, self.ebg], [self.Cst])
        self.cp("scalar", self.Cbf[:, :, :], self.Cst[:, :, :], [self.Cst], [self.Cbf])
        if full:
            hss = self.hss
            self.memset("vector", hss[:, 0:4], 0.0, [hss])
            for hh in range(4):
                junk = self.junk2
                self.act(junk[:, 0:128], self.hg[:, hh * 128:(hh + 1) * 128], AF.Square, [self.hg], [junk, hss],
                         accum_out=hss[:, hh:hh + 1])
            self.ts("vector", hss[:, 4:8], hss[:, 0:4], 1.0 / 128, 1e-6, ALU.mult, ALU.add, [hss], [hss])
            self.tt("gpsimd", hss[:, 8:12], hss[:, 4:8], self.consts[:, 2:3].to_broadcast([128, 4]), ALU.pow,
                    [hss, self.consts], [hss])
            ym = self.tb()
            for hh in range(4):
                self.stt(ym[:, hh * 128:(hh + 1) * 128], self.hg[:, hh * 128:(hh + 1) * 128], hss[:, 8 + hh:9 + hh],
                         self.mng[:, hh * 128:(hh + 1) * 128], ALU.mult, ALU.mult, [self.hg, hss, self.mng], [ym])
            tb2 = self.tbank()
            for hh in range(4):
                self.tr(tb2[:, hh * 128:(hh + 1) * 128], ym[:, hh * 128:(hh + 1) * 128], self.identb[:], [ym, self.identb], [tb2])
            self.cp("scalar", self.ymT[:, :, cs], tb2[:, 0:512].rearrange("p (k t) -> p k t", k=4), [tb2], [self.ymT])

    def s5_bounce(self, N):
        J = N // 8
        usT, Uall, X0, Xp = self.usT, self.Uall, self.X0, self.Xp
        if J == 64:
            self.dma(self.bq, self.scrA.t.rearrange("(a p) s j -> p a s j", p=128)[:, :, :, 0:J], usT[:, :, :, 0:J],
                     [usT], [self.scrA])
        else:
            for a_ in range(4):
                self.dma(self.bq, self.scrA.t[a_ * 128:(a_ + 1) * 128, :, 0:J], usT[:, a_, :, 0:J], [usT], [self.scrA])
        src = self.scrA.t.rearrange("(g i) s j -> s i g j", i=16)
        for s in range(8):
            self.dma(self.bq, Uall[s * 16:(s + 1) * 16, :, 0:J], src[s][:, :, 0:J], [self.scrA], [Uall])

    def s5_states(self, N, full):
        J = N // 8
        usT, Uall, X0, Xp = self.usT, self.Uall, self.X0, self.Xp
        for q in range(4):
            pb = self.bank()
            for pl in range(4):
                pr = q * 4 + pl
                for e in range(2):
                    g = pr * 2 + e
                    for ri, Ws in enumerate((self.WsRe, self.WsIm)):
                        c0 = (pl * 2 + ri) * 64
                        self.mm(pb[e * 64:(e + 1) * 64, c0:c0 + J], Ws[:, g, :], Uall[:, g, 0:J], True, True, [Ws, Uall], [pb])
            self.cp("scalar", X0[:, q * 4:(q + 1) * 4, :, 0:J],
                    pb[:, :].rearrange("p (a e j) -> p a e j", a=4, e=2)[:, :, :, 0:J], [pb], [X0])
        yield
        if full:
            yield from self.scan_full(J)
        else:
            self.dump("Xs", self.X0, self.X0[:, :, :, 0:J])
            self.scan_reduce(J)
            self.dump("Xr", self.X0, self.X0[:, :, :, 0:J])
            self.dump("carryR", self.carry, self.carry[:, :, :])

    def _cmul_small(self, outr, outi, ar, ai, xr, xi, rd, wr):
        t = self.tf()
        tv = t[:, 0:64].rearrange("p (k a) -> p k a", k=4)
        self.tt("vector", tv[:, 0, :], ar, xr, ALU.mult, rd, [t])
        self.tt("vector", tv[:, 1, :], ai, xi, ALU.mult, rd, [t])
        self.tt("vector", tv[:, 2, :], ar, xi, ALU.mult, rd, [t])
        self.tt("vector", tv[:, 3, :], ai, xr, ALU.mult, rd, [t])
        self.tt("vector", outr, tv[:, 0, :], tv[:, 1, :], ALU.subtract, [t], wr)
        self.tt("vector", outi, tv[:, 2, :], tv[:, 3, :], ALU.add, [t], wr)

    def scan_full(self, J):
        X0, Xp = self.X0, self.Xp
        t = self.tf()
        tv = t[:, 64:128].rearrange("p (k a) -> p k a", k=4)
        self._cmul_small(tv[:, 0, :], tv[:, 1, :], self.cpr[:, 0, :], self.cpi[:, 0, :], self.carry[:, :, 0], self.carry[:, :, 1],
                         [self.cpr, self.cpi, self.carry], [t])
        self.tt("vector", X0[:, :, 0, 0], X0[:, :, 0, 0], tv[:, 0, :], ALU.add, [X0, t], [X0])
        self.tt("vector", X0[:, :, 1, 0], X0[:, :, 1, 0], tv[:, 1, :], ALU.add, [X0, t], [X0])
        self.cp("scalar", Xp[:, :, :, 0], self.carry[:, :, :], [self.carry], [Xp])
        k = 0
        dsh = 1
        A, B = self.stA, self.stB
        while dsh < J:
            n = J - dsh
            for hf in range(2):
                ps_ = slice(hf * 8, (hf + 1) * 8)
                a1 = A[:, 0:16 * n].rearrange("p (a e j) -> p a e j", a=8, e=2)
                a2 = B[:, 0:16 * n].rearrange("p (a e j) -> p a e j", a=8, e=2)
                xr, xi = X0[:, ps_, 0, 0:n], X0[:, ps_, 1, 0:n]
                ar3 = self.cpr[:, k, ps_].unsqueeze(2).to_broadcast([128, 8, n])
                ai3 = self.cpi[:, k, ps_].unsqueeze(2).to_broadcast([128, 8, n])
                self.tt("vector", a2[:, :, 1, :], xi, ai3, ALU.mult, [X0.s(hf, 1), self.cpi], [B.s("im")])
                self.tt("vector", a1[:, :, 0, :], xr, ar3, ALU.mult, [X0.s(hf, 0), self.cpr], [A.s("re")])
                self.tt("vector", a2[:, :, 0, :], xr, ai3, ALU.mult, [X0.s(hf, 0), self.cpi], [B.s("re")])
                self.tt("vector", a1[:, :, 1, :], xi, ar3, ALU.mult, [X0.s(hf, 1), self.cpr], [A.s("im")])
                self.tt("vector", a1[:, :, 0, :], a1[:, :, 0, :], a2[:, :, 1, :], ALU.subtract, [A.s("re"), B.s("im")], [A.s("re")])
                self.tt("vector", a1[:, :, 1, :], a1[:, :, 1, :], a2[:, :, 0, :], ALU.add, [A.s("im"), B.s("re")], [A.s("im")])
                self.tt("vector", X0[:, ps_, 0, dsh:J], X0[:, ps_, 0, dsh:J], a1[:, :, 0, :], ALU.add, [X0.s(hf, 0), A.s("re")], [X0.s(hf, 0)])
                self.tt("vector", X0[:, ps_, 1, dsh:J], X0[:, ps_, 1, dsh:J], a1[:, :, 1, :], ALU.add, [X0.s(hf, 1), A.s("im")], [X0.s(hf, 1)])
                yield
            dsh *= 2
            k += 1
        if J > 1:
            self.cp("scalar", Xp[:, :, :, 1:J], X0[:, :, :, 0:J - 1], [X0], [Xp])
        self.cp("vector", self.carry[:, :, :], X0[:, :, :, J - 1], [X0], [self.carry])

    def scan_reduce(self, J):
        X0 = self.X0
        A, B = self.stA, self.stB
        xe = lambda e: X0.s(0, e) + X0.s(1, e)
        k = 0
        n = J // 2
        while n >= 1:
            a1 = A[:, 0:32 * n].rearrange("p (a e j) -> p a e j", a=16, e=2)
            a2 = B[:, 0:32 * n].rearrange("p (a e j) -> p a e j", a=16, e=2)
            evr, evi = X0[:, :, 0, 0:2 * n:2], X0[:, :, 1, 0:2 * n:2]
            odr, odi = X0[:, :, 0, 1:2 * n:2], X0[:, :, 1, 1:2 * n:2]
            ar3 = self.cpr[:, k, :].unsqueeze(2).to_broadcast([128, 16, n])
            ai3 = self.cpi[:, k, :].unsqueeze(2).to_broadcast([128, 16, n])
            self.tt("vector", a1[:, :, 0, :], evr, ar3, ALU.mult, [xe(0), self.cpr], [A.s("re")])
            self.tt("vector", a2[:, :, 1, :], evi, ai3, ALU.mult, [xe(1), self.cpi], [B.s("im")])
            self.tt("vector", a2[:, :, 0, :], evr, ai3, ALU.mult, [xe(0), self.cpi], [B.s("re")])
            self.tt("vector", a1[:, :, 1, :], evi, ar3, ALU.mult, [xe(1), self.cpr], [A.s("im")])
            self.tt("vector", a1[:, :, 0, :], a1[:, :, 0, :], a2[:, :, 1, :], ALU.subtract, [A.s("re"), B.s("im")], [A.s("re")])
            self.tt("vector", a1[:, :, 1, :], a1[:, :, 1, :], a2[:, :, 0, :], ALU.add, [A.s("im"), B.s("re")], [A.s("im")])
            self.tt("vector", a1[:, :, 0, :], a1[:, :, 0, :], odr, ALU.add, [A.s("re"), xe(0)], [A.s("re")])
            self.tt("vector", a1[:, :, 1, :], a1[:, :, 1, :], odi, ALU.add, [A.s("im"), xe(1)], [A.s("im")])
            self.cp("vector", X0[:, :, 0, 0:n], a1[:, :, 0, :], [A.s("re")], [xe(0)])
            self.cp("vector", X0[:, :, 1, 0:n], a1[:, :, 1, :], [A.s("im")], [xe(1)])
            n //= 2
            k += 1
        t = self.tf()
        tv = t[:, 64:128].rearrange("p (k a) -> p k a", k=4)
        self._cmul_small(tv[:, 0, :], tv[:, 1, :], self.cpr[:, k, :], self.cpi[:, k, :], self.carry[:, :, 0], self.carry[:, :, 1],
                         [self.cpr, self.cpi, self.carry], [t])
        self.tt("vector", self.carry[:, :, 0], tv[:, 0, :], X0[:, :, 0, 0], ALU.add, [t, X0], [self.carry])
        self.tt("vector", self.carry[:, :, 1], tv[:, 1, :], X0[:, :, 1, 0], ALU.add, [t, X0], [self.carry])

    def s5_out(self, N, glu_w):
        J = N // 8
        Uall, Xp, Ygl = self.Uall, self.Xp, self.Ygl
        for q in range(4):
            pb = self.bank()
            for gl in range(8):
                g = q * 8 + gl
                pr, e = g // 2, g % 2
                o = pb[:, gl * 64:gl * 64 + J]
                self.mm(o, self.Mw[:, g, :], Uall[:, g, 0:J], True, False, [self.Mw, Uall], [pb])
                self.mm(o, self.VwRe[e * 64:(e + 1) * 64, pr, :], Xp[e * 64:(e + 1) * 64, pr, 0, 0:J], False, False,
                        [self.VwRe, Xp], [pb])
                self.mm(o, self.VwIm[e * 64:(e + 1) * 64, pr, :], Xp[e * 64:(e + 1) * 64, pr, 1, 0:J], False, True,
                        [self.VwIm, Xp], [pb])
            pv = pb[:, :].rearrange("p (g j) -> p g j", g=8)[:, :, 0:J]
            self.gelu_from(pv, Ygl[:, q * 8:(q + 1) * 8, 0:J], [pb], [Ygl], [128, 8, J])
            yield
        dst = self.scrB.t.rearrange("(g o) r j -> r o g j", o=16)
        for r in range(8):
            self.dma(self.bq, dst[r][:, :, 0:J], Ygl[r * 16:(r + 1) * 16, :, 0:J], [Ygl], [self.scrB])
        ygT = self.usT
        if J == 64:
            self.dma(self.bq, ygT[:, :, :, 0:J], self.scrB.t.rearrange("(a p) r j -> p a r j", p=128)[:, :, :, 0:J],
                     [self.scrB], [ygT])
        else:
            for a_ in range(4):
                self.dma(self.bq, ygT[:, a_, :, 0:J], self.scrB.t[a_ * 128:(a_ + 1) * 128, :, 0:J], [self.scrB], [ygT])
        def yv(k):
            if J == 64:
                return ygT[:, k, :, :].rearrange("p r j -> p (r j)")
            return None
        for mt in range(4):
            pb = self.bank()
            if J == 64:
                for k in range(4):
                    self.mm(pb[:, 0:N], glu_w[:, k, mt * 128:(mt + 1) * 128], yv(k), k == 0, k == 3, [self.wglu_slot, ygT], [pb])
            else:
                for r in range(8):
                    for k in range(4):
                        self.mm(pb[:, r * J:(r + 1) * J], glu_w[:, k, mt * 128:(mt + 1) * 128], ygT[:, k, r, 0:J], k == 0, k == 3,
                                [self.wglu_slot, ygT], [pb])
            th = self.tf()
            self.act(th[:, 0:N], pb[:, 0:N], AF.Tanh, [pb, self.bgluT], [th], scale=0.5, bias=self.bgluT[:, mt:mt + 1])
            ov = self.ys5T[:, mt, 0:N].rearrange("p (r j) -> p r j", r=8)
            self.stt(ov, th[:, 0:N].rearrange("p (r j) -> p r j", r=8), 1.0, ygT[:, mt, :, 0:J], ALU.add, ALU.mult,
                     [th, ygT], [self.ys5T])
            yield

    def gelu_from(self, pin, out, r, w, shp):
        n = int(np.prod(shp[1:]))

        def vw(t):
            ap = t[:, 0:n]
            if len(shp) == 3:
                ap = ap.rearrange("p (a b) -> p a b", a=shp[1])
            return ap
        xh, sq, th = self.tf(), self.tf(), self.tf()
        self.act(vw(xh), pin, AF.Copy, r, [xh], scale=0.5)
        self.gelu_half(vw(xh), xh, vw(sq), sq, vw(th), th, out, w)

    def gelu_half(self, xh, xh_t, sq, sq_t, th, th_t, out, w, mul=None, mul_t=None):
        self.act(sq, xh, AF.Square, [xh_t], [sq_t], scale=math.sqrt(2 * GC * 4 * 0.044715))
        self.stt(th, sq, 2 * GC, xh, ALU.add, ALU.mult, [sq_t, xh_t], [th_t])
        self.act(sq, th, AF.Tanh, [th_t], [sq_t])
        if mul is None:
            self.stt(out, sq, 1.0, xh, ALU.add, ALU.mult, [sq_t, xh_t], w)
        else:
            self.stt(th, sq, 1.0, xh, ALU.add, ALU.mult, [sq_t, xh_t], [th_t])
            self.tt("vector", out, th, mul, ALU.mult, [th_t, mul_t], w)

    def resid_evac(self, pb, c, half, scale):
        dst = self.xt[:, c, half * 512:(half + 1) * 512]
        self.stt(dst, pb[:, 0:512], scale, dst, ALU.mult, ALU.add, [pb, self.xt.s(c, half)], [self.xt.s(c, half)])

    def proj_tokmajor_resid(self, srcT, kt, wname, nch, scale):
        for half in range(2):
            slot, wv = self.wslab(self.wdep(wname, half * 512, (half + 1) * 512), self.w3(wname, half * 512, (half + 1) * 512)[:, 0:kt, :], kt, 512)
            for c in range(nch):
                pb = self.bank()
                self.tm_matmul(pb, srcT, c, wv, 0, 512, kt, [srcT, slot])
                self.resid_evac(pb, c, half, scale)
                yield

    def load_x(self, xt, tok0, N):
        self.dma(self.bq, xt[:, 0:N // 128, :], self.xs[tok0:tok0 + N, :].rearrange("(c p) d -> p c d", p=128), [self.xs], [xt])

    def tile(self, tok0, N, full, out_row, first=True, nxt=None, ahead=None):
        self.wphase = "front"
        nch = N // 128
        J = N // 8
        cg0 = tok0 // 128
        hT = self.hT
        if ahead is not None:
            specs_, i_ = ahead
            p_own = 0 if self.xt is self.xtb[0] else 1
            p_oth = 1 - p_own
            if i_ == 0:
                self.load_x(self.xtb[p_own], tok0, N)
                if len(specs_) > 1:
                    self.load_x(self.xtb[p_oth], specs_[1][0], specs_[1][1])
                self.norm_T(nch, self.g1T)
                if len(specs_) > 2:
                    self.load_x(self.xtb[p_own], specs_[2][0], specs_[2][1])
            if i_ + 1 < len(specs_):
                self.set_par(p_oth)
                self.norm_T(specs_[i_ + 1][1] // 128, self.g1T)
                self.set_par(p_own)
                if i_ + 3 < len(specs_):
                    self.load_x(self.xtb[p_oth], specs_[i_ + 3][0], specs_[i_ + 3][1])
            self.precast_flush(self.pc_per, [self.rstd])
        else:
            if first:
                self.load_x(self.xt, tok0, N)
            if nxt is not None:
                other = self.xtb[1] if self.xt is self.xtb[0] else self.xtb[0]
                self.load_x(other, nxt[0], nxt[1])
        if full and N == 512:
            self.dump("carry0", self.carry, self.carry[:, :, :])
            self.dump("Cst0", self.Cst, self.Cst[:, :, :])
        if ahead is None:
            self.norm_T(nch, self.g1T)
        if full:
            self.dump("hT", self.hT, self.hT[:, :, 0:N])
        wi = None
        if full:
            uslot, uw = self.wslab(self.wdep("w_in", 0, 512), self.w3("w_in", 0, 512), 8, 512)
        else:
            uslot, uw = self.res_u
        for mt in range(4):
            pb = self.bank()
            self.fm_matmul(pb, uw, mt * 128, hT, 8, N, [uslot, hT])
            self.cp("scalar", self.usT[:, mt, :, 0:J], pb[:, 0:N].rearrange("p (j s) -> p s j", s=8), [pb], [self.usT])
            yield
        if full:
            self.dump("usT", self.usT, self.usT[:, :, :, 0:J])
        self.s5_bounce(N)
        if full:
            qslot, qw = self.wslab(self.wdep("w_in", 512, 1024), self.w3("w_in", 512, 1024), 8, 512)
            for mp in (0, 2):
                items = []
                for mt in (mp, mp + 1):
                    pb = self.bank()
                    self.fm_matmul(pb, qw, mt * 128, hT, 8, N, [qslot, hT])
                    items.append((pb, mt, mt))
                self.conv_silu2(items, N)
                yield
            kslot, kw = self.wslab(self.wdep("w_in", 1024, 1536), self.w3("w_in", 1024, 1536), 8, 512)
        else:
            kslot, kw = self.res_k
        for mp in (0, 2):
            items = []
            for mt in (mp, mp + 1):
                pb = self.bank()
                self.fm_matmul(pb, kw, mt * 128, hT, 8, N, [kslot, hT])
                items.append((pb, 4 + mt, self.kbase + mt))
            self.conv_silu2(items, N)
            yield
        if full:
            vslot, vw = self.wslab(self.wdep("w_in", 1536, 2048), self.w3("w_in", 1536, 2048), 8, 512)
        else:
            vslot, vw = self.res_v
        for c in range(nch):
            pb = self.bank()
            self.tm_matmul(pb, hT, c, vw, 0, 512, 8, [hT, vslot])
            self.cp("scalar", self.vext[:, c, :, 0:128], pb[:, 0:512].rearrange("p (h d) -> p h d", h=4), [pb], [self.vext])
            yield
        if full:
            oslot, ow = self.wslab(self.wdep("w_in", 2048, 2560), self.w3("w_in", 2048, 2560), 8, 512)
            for c in range(nch):
                pb = self.bank()
                self.tm_matmul(pb, hT, c, ow, 0, 512, 8, [hT, oslot])
                th = self.tf()
                self.act(th[:, 0:512], pb[:, 0:512], AF.Tanh, [pb], [th], scale=0.5)
                self.ts("vector", self.sgo[:, c, :], th[:, 0:512], 0.5, 0.5, ALU.mult, ALU.add, [th], [self.sgo])
                yield
        self.gates_all(nch, cg0, fold=not full)
        if not full:
            yield "MARK"
            yield from self.s5_states(N, full)
            yield from self.mlstm_prefix(nch)
            return
        self.memset("vector", self.vext[:, :, :, 128:129], 1.0, [self.vext])
        for c in range(nch):
            self.mlstm_chunk(c, full)
            yield
        self.dump("ymT", self.ymT, self.ymT[:, :, 0:N])
        for hf in range(2):
            mslot, mw = self.wslab(self.wdep("w_br_m", hf * 512, (hf + 1) * 512), self.w3("w_br_m", hf * 512, (hf + 1) * 512), 4, 512)
            g2slot, g2w = self.wslab(self.wdep("w_in", 2568 + 1024 + hf * 512, 2568 + 1024 + (hf + 1) * 512),
                                     self.w3("w_in", 2568 + 1024 + hf * 512, 2568 + 1024 + (hf + 1) * 512), 8, 512)
            for ml in range(4):
                mt = hf * 4 + ml
                pg = self.bank()
                self.fm_matmul(pg, g2w, ml * 128, hT, 8, N, [g2slot, hT])
                th = self.tf()
                self.act(th[:, 0:N], pg[:, 0:N], AF.Tanh, [pg, self.bgT], [th], scale=0.5, bias=self.bgT[:, 8 + mt:9 + mt])
                pbr = self.bank()
                self.fm_matmul(pbr, mw, ml * 128, self.ymT, 4, N, [mslot, self.ymT])
                self.stt(self.mergedT[:, mt, 0:N], th[:, 0:N], 1.0, pbr[:, 0:N], ALU.add, ALU.mult, [th, pbr], [self.mergedT])
                yield
        yield from self.s5_states(N, full)
        gslot, gw = self.wslab(self.wdep("s5_w_glu", 0, 512), self.w3("s5_w_glu", 0, 512), 4, 512)
        self.wglu_slot = gslot
        yield from self.s5_out(N, gw)
        self.dump("ys5T", self.ys5T, self.ys5T[:, :, 0:N])
        for hf in range(2):
            s5slot, s5w = self.wslab(self.wdep("w_br_s5", hf * 512, (hf + 1) * 512), self.w3("w_br_s5", hf * 512, (hf + 1) * 512), 4, 512)
            g1slot, g1w = self.wslab(self.wdep("w_in", 2568 + hf * 512, 2568 + (hf + 1) * 512),
                                     self.w3("w_in", 2568 + hf * 512, 2568 + (hf + 1) * 512), 8, 512)
            for ml in range(4):
                mt = hf * 4 + ml
                pg = self.bank()
                self.fm_matmul(pg, g1w, ml * 128, hT, 8, N, [g1slot, hT])
                th = self.tf()
                self.act(th[:, 0:N], pg[:, 0:N], AF.Tanh, [pg, self.bgT], [th], scale=0.5, bias=self.bgT[:, mt:mt + 1])
                pbr = self.bank()
                self.fm_matmul(pbr, s5w, ml * 128, self.ys5T, 4, N, [s5slot, self.ys5T])
                m1 = self.tf()
                self.stt(m1[:, 0:N].rearrange("p (j r) -> p j r", r=8), th[:, 0:N].rearrange("p (j r) -> p j r", r=8), 1.0,
                         pbr[:, 0:N].rearrange("p (r j) -> p j r", r=8), ALU.add, ALU.mult, [th, pbr], [m1])
                self.stt(self.mergedT[:, mt, 0:N], m1[:, 0:N], 0.5, self.mergedT[:, mt, 0:N], ALU.mult, ALU.add,
                         [m1, self.mergedT], [self.mergedT])
                yield
        self.dump("mergedT", self.mergedT, self.mergedT[:, :, 0:N])
        yield from self.proj_tokmajor_resid(self.mergedT, 8, "w_out", nch, 0.5)
        if self.dbg == "x1":
            return self.store_xt(nch, N, out_row)
        self.norm_T(nch, self.g2T)
        qxT = self.qkT
        for hf in range(2):
            slot, wv = self.wslab(self.wdep("x_wq", hf * 512, (hf + 1) * 512), self.w3("x_wq", hf * 512, (hf + 1) * 512), 8, 512)
            for ml in range(4):
                pb = self.bank()
                self.fm_matmul(pb, wv, ml * 128, hT, 8, N, [slot, hT])
                self.cp("scalar", qxT[:, hf * 4 + ml, 0:N], pb[:, 0:N], [pb], [qxT])
                yield
        oxT = self.mergedT
        for hh in range(4):
            PT = self.tb()
            for mtile in range(2):
                pb = self.bank()
                for dd in range(2):
                    self.mm(pb[:, 0:N], self.kxT[:, hh * 2 + dd, mtile * 128:(mtile + 1) * 128], qxT[:, hh * 2 + dd, 0:N],
                            dd == 0, dd == 1, [self.kxT, qxT], [pb])
                self.act(PT[:, mtile * 512:mtile * 512 + N], pb[:, 0:N], AF.Exp, [pb], [PT], scale=1.0 / 16)
            pbs = self.bank()
            for mtile in range(2):
                self.mm(pbs[:, 0:N], self.onesb[:], PT[:, mtile * 512:mtile * 512 + N], mtile == 0, mtile == 1, [self.onesb, PT], [pbs])
            rec = self.tf()
            self.V(lambda E, rec=rec, pbs=pbs: E.reciprocal(out=rec[:, 0:N], in_=pbs[:, 0:N]), [pbs], [rec])
            for dvt in range(2):
                pbo = self.bank()
                for mtile in range(2):
                    c0 = hh * 256 + dvt * 128
                    self.mm(pbo[:, 0:N], self.vx[:, mtile, c0:c0 + 128], PT[:, mtile * 512:mtile * 512 + N],
                            mtile == 0, mtile == 1, [self.vx, PT], [pbo])
                self.tt("vector", oxT[:, hh * 2 + dvt, 0:N], pbo[:, 0:N], rec[:, 0:N], ALU.mult, [pbo, rec], [oxT])
            yield
        yield from self.proj_tokmajor_resid(oxT, 8, "x_wo", nch, 1.0)
        if self.dbg == "x2":
            return self.store_xt(nch, N, out_row)
        self.norm_T(nch, self.g3T)
        yield "MARK"
        self.wphase = "back"
        for (i0, i1) in ((0, 3), (3, 6), (6, 9), (9, 11)):
            mt0 = i0 * 2
            nk = (i1 - i0) * 2
            for i in range(i0, i1):
                self.wphase = "back"
                aslot, aw = self.wslab(self.wdep("f_w_up", i * 256, (i + 1) * 256), self.w3("f_w_up", i * 256, (i + 1) * 256), 8, 256, half=0)
                bslot, bw = self.wslab(self.wdep("f_w_up", 2816 + i * 256, 2816 + (i + 1) * 256),
                                       self.w3("f_w_up", 2816 + i * 256, 2816 + (i + 1) * 256), 8, 256, half=1)
                st1 = []
                for e in range(2):
                    mt = i * 2 + e
                    up_s = self.uprev.s(mt % 2)
                    pbs_ = []
                    for (slot, wv, ch) in ((aslot, aw, mt), (bslot, bw, 22 + mt)):
                        pb = self.bank()
                        self.fm_matmul(pb, wv, e * 128, hT, 8, N, [slot, hT])
                        pbs_.append((pb, ch))
                    if out_row is None:
                        for pb, ch in pbs_:
                            self.cp("scalar", self.uprev[:, ch, :], pb[:, N - 2:N], [pb], [up_s])
                        continue
                    raws = [self.tf(), self.tf()]
                    accs = [self.tf(), self.tf()]
                    for j_, (pb, ch) in enumerate(pbs_):
                        self.cp("scalar", raws[j_][:, 0:2], self.uprev[:, ch, :], [up_s], [raws[j_]])
                        self.cp("scalar", raws[j_][:, 2:2 + N], pb[:, 0:N], [pb], [raws[j_]])
                        self.act(accs[j_][:, 0:N], pb[:, 0:N], AF.Identity, [pb, self.fcw, self.fcb], [accs[j_]],
                                 scale=self.fcw[:, ch, 2:3], bias=self.fcb[:, ch:ch + 1])
                        self.cp("scalar", self.uprev[:, ch, :], raws[j_][:, N:N + 2], [raws[j_]], [up_s])
                    for k in (1, 0):
                        for j_, (pb, ch) in enumerate(pbs_):
                            self.stt(accs[j_][:, 0:N], raws[j_][:, k:k + N], self.fcw[:, ch, k:k + 1], accs[j_][:, 0:N], ALU.mult, ALU.add,
                                     [raws[j_], self.fcw, accs[j_]], [accs[j_]])
                    st1.append((mt, raws, accs))
                for (mt, raws, accs) in st1:
                    ah, bcv = accs
                    ra, rb = raws
                    self.act(rb[:, 0:N], ah[:, 0:N], AF.Square, [ah], [rb], scale=math.sqrt(2 * GC * 4 * 0.044715))
                    self.tt("vector", ra[:, 0:N], ah[:, 0:N], bcv[:, 0:N], ALU.mult, [ah, bcv], [ra])
                    self.stt(bcv[:, 0:N], rb[:, 0:N], 2 * GC, ah[:, 0:N], ALU.add, ALU.mult, [rb, ah], [bcv])
                    self.act(rb[:, 0:N], bcv[:, 0:N], AF.Tanh, [bcv], [rb])
                    self.stt(self.gT[:, mt - mt0, 0:N], rb[:, 0:N], 1.0, ra[:, 0:N], ALU.add, ALU.mult, [rb, ra], [self.gT])
                yield
            if out_row is None:
                continue
            for half in range(2):
                self.wphase = "back"
                pbs = [self.bank() for _ in range(nch)]
                srcw = self.wb["f_w_down"].t[mt0 * 128:(mt0 + nk) * 128, half * 512:(half + 1) * 512].rearrange("(k p) n -> p k n", p=128)
                slot, wv = self.wslab(self.wdep("f_w_down", half * 512, (half + 1) * 512), srcw, nk, 512, q=self.bq)
                for c in range(nch):
                    for k in range(nk):
                        self.mm(pbs[c][:, 0:512], self.gT[:, k, c * 128:(c + 1) * 128], wv[:, k, :],
                                k == 0, k == nk - 1, [self.gT, slot], [pbs[c]])
                for c in range(nch):
                    self.resid_evac(pbs[c], c, half, 1.0)
                yield
        if out_row is None:
            self.act(self.uprev[:], self.uprev[:], AF.Copy, [self.uprev, self.hvalid], [self.uprev], scale=self.hvalid[:, 0:1])
            return
        if self.dbg == "x3":
            return self.store_xt(nch, N, out_row)
        xt, ss, rstd = self.xt, self.ss, self.rstd
        self.memset("vector", ss[:], 0.0, [ss])
        for c in range(nch):
            junk = self.tb()
            self.act(junk[:], xt[:, c, :], AF.Square, [xt.s(c)], [junk, ss], accum_out=ss[:, c:c + 1])
        self.ts("vector", rstd[:, 0:nch], ss[:, 0:nch], 1.0 / D, 1e-6, ALU.mult, ALU.add, [ss], [rstd])
        self.tt("gpsimd", rstd[:, 0:nch], rstd[:, 0:nch], self.consts[:, 2:3].to_broadcast([128, nch]), ALU.pow,
                [rstd, self.consts], [rstd])
        for half in range(2):
            gf = self.tf()
            self.dma("gpsimd", gf[:, 0:512], self.din["final_norm_g"].t[half * 512:(half + 1) * 512].partition_broadcast(128), [], [gf])
            for c in range(nch):
                dst = xt[:, c, half * 512:(half + 1) * 512]
                self.stt(dst, dst, rstd[:, c:c + 1], gf[:, 0:512], ALU.mult, ALU.mult, [xt.s(c, half), rstd, gf], [xt.s(c, half)])
        self.dma("gpsimd", self.out[out_row:out_row + N, :].rearrange("(c p) d -> p c d", p=128), xt[:, 0:nch, :], [xt], [self.out])
        yield

    def store_xt(self, nch, N, out_row):
        if out_row is not None:
            self.dma("sync", self.out[out_row:out_row + N, :].rearrange("(c p) d -> p c d", p=128), self.xt[:, 0:nch, :],
                     [self.xt], [self.out])

    def dump(self, name, src_t, src_ap):
        if not self.dbg or name in self.dumps:
            return
        o = self.dram("dbg_" + name, [int(x) for x in src_ap.shape], F32, "ExternalOutput")
        self.dma("gpsimd", o.t, src_ap, [src_t], [o])
        self.dumps[name] = o

    def mem_kv(self):
        self.dma("sync", self.xt[:, 0:2, :], self.mem.t.rearrange("(c p) d -> p c d", p=128), [self.mem], [self.xt])
        self.norm_T(2, self.gmT)
        for hf in range(2):
            slot, wv = self.wslab(self.wdep("x_wkv", hf * 512, (hf + 1) * 512), self.w3("x_wkv", hf * 512, (hf + 1) * 512), 8, 512)
            for ml in range(4):
                pb = self.bank()
                self.fm_matmul(pb, wv, ml * 128, self.hT, 8, 256, [slot, self.hT])
                self.cp("vector", self.kxT[:, hf * 4 + ml, :], pb[:, 0:256], [pb], [self.kxT])
        for hf in range(2):
            slot, wv = self.wslab(self.wdep("x_wkv", 1024 + hf * 512, 1024 + (hf + 1) * 512), self.w3("x_wkv", 1024 + hf * 512, 1024 + (hf + 1) * 512), 8, 512)
            for c in range(2):
                pb = self.bank()
                self.tm_matmul(pb, self.hT, c, wv, 0, 512, 8, [self.hT, slot])
                self.cp("vector", self.vx[:, c, hf * 512:(hf + 1) * 512], pb[:, 0:512], [pb], [self.vx])

    def run_pipeline(self, specs, pipelined=True, K=3, alt=False):
        gens = [{"gen": self.tile(*sp, first=True, nxt=None, ahead=((specs, i) if alt else None)),
                 "par": (i + 1) % 2, "par2": (i % 2) if alt else 0, "phase": "front",
                 "bq": ("gpsimd" if (sp[2] and sp[3] is not None) else "sync")}
                for i, sp in enumerate(specs)]

        def step(g):
            self.set_par(g["par"])
            self.set_par2(g["par2"])
            self.wphase = g["phase"]
            self.bq = g["bq"]
            try:
                r = next(g["gen"])
            except StopIteration:
                return "END"
            if r == "MARK":
                g["phase"] = "back"
            return r

        if not pipelined:
            for g in gens:
                while step(g) != "END":
                    pass
            return
        active = None
        for g in gens:
            if active is None:
                active = g
                while True:
                    r = step(active)
                    if r in ("MARK", "END"):
                        break
                if r == "END":
                    active = None
                continue
            g_state = None
            while True:
                r = step(active)
                if r == "END":
                    break
                if g_state is None:
                    for _ in range(K):
                        r2 = step(g)
                        if r2 in ("MARK", "END"):
                            g_state = r2
                            break
            if g_state is None:
                while True:
                    r2 = step(g)
                    if r2 in ("MARK", "END"):
                        g_state = r2
                        break
            active = None if g_state == "END" else g
        if active is not None:
            while step(active) != "END":
                pass

    def build(self, n_pre_tiles=12, n_main_tiles=4, do_halo=True, pipelined=True):
        self.dumps = {}
        self.declare_io()
        self.wblocks = {}
        self.pc_jobs = []
        self.alloc_core()
        self.precast_cols("w_in", [(0, 512), (1024, 1536), (1536, 2048), (2560, 2568)])
        d = self.din
        self.dma("sync", self.identf[:], d["c_ident"][:, :], [], [self.identf])
        self.memset("vector", self.consts[:, 0:1], 1.0, [self.consts])
        self.memset("vector", self.consts[:, 1:2], math.pi / 2, [self.consts])
        self.memset("vector", self.consts[:, 2:3], -0.5, [self.consts])
        self.memset("vector", self.consts[:, 3:4], 0.0, [self.consts])
        self.s5_setup()
        self.alloc_work()
        self.load_consts()
        self.precast_cols("w_in", [(512, 1024), (2048, 2560)], True)
        self.precast_cols("s5_w_glu", [(0, 512)], True)
        self.precast_cols("x_wkv", [(i * 512, (i + 1) * 512) for i in range(4)], True)
        self.precast_cols("w_br_s5", [(0, 512), (512, 1024)], True)
        self.precast_cols("w_in", [(2568 + i * 512, 2568 + (i + 1) * 512) for i in range(4)], True)
        self.precast_cols("w_br_m", [(0, 512), (512, 1024)], True)
        for nm_ in ("w_out", "x_wq", "x_wo"):
            self.precast_cols(nm_, [(0, 512), (512, 1024)], True)
        self.precast_cols("f_w_up", [(i * 512, (i + 1) * 512) for i in range(11)], True)
        self.precast_cols("f_w_down", [(0, 512), (512, 1024)], True)
        self.dma("sync", self.ifslab[:], self.w3("w_in", 2560, 2568), self.wdep("w_in", 2560, 2568), [self.ifslab])
        if n_pre_tiles:
            self.res_u = self.wslab(self.wdep("w_in", 0, 512), self.w3("w_in", 0, 512), 8, 512, slot=0)
            self.res_k = self.wslab(self.wdep("w_in", 1024, 1536), self.w3("w_in", 1024, 1536), 8, 512, slot=1)
            self.res_v = self.wslab(self.wdep("w_in", 1536, 2048), self.w3("w_in", 1536, 2048), 8, 512, slot=2)
        n_early = len([j for j in self.pc_jobs if not j[0].startswith("f_w_")])
        self.pc_per = (n_early + max(n_pre_tiles, 1) - 1) // max(n_pre_tiles, 1)
        pre = [(i * 512, 512, False, None) for i in range(12 - n_pre_tiles, 12)]
        self.run_pipeline(pre, pipelined, K=4, alt=True)
        self.set_par2(0)
        self.precast_flush(1000)
        self.set_par(0)
        self.wphase = "front"
        self.mem_kv()
        specs = []
        if do_halo:
            specs.append((NPRE, 128, True, None))
        for i in range(n_main_tiles):
            specs.append((NPRE + NHALO + i * 512, 512, True, i * 512))
        self.run_pipeline(specs, pipelined)
        self.mk.wait_bufs("sync", [self.out] + list(self.dumps.values()))
        self.mk.emit()
        self.mk.close()


def build_nc(dbg=None, **kw):
    nc = bass.Bass("TRN2", target_bir_lowering=False)
    es = contextlib.ExitStack()
    kb = KB(nc, es, dbg)
    with es:
        kb.build(**kw)
    return nc, kb


def make_consts():
    idx = np.arange(128)
    c = {
        "c_ident": np.eye(128, dtype=np.float32),
        "c_tri": (idx[:, None] <= idx[None, :]).astype(np.float32),
        "c_ones": np.ones((128, 128), np.float32),
        "c_mmask": ((idx[None, :] // 16) >= (idx[:, None] // 16)).astype(np.float32),
    }
    return c


def core_inputs(inputs, core):
    b, s = core // 4, core % 4
    x = inputs["x"]
    xs = np.zeros((NTOK, D), np.float32)
    valid = np.zeros((NTOK,), np.float32)
    end = (s + 1) * SEG
    start = end - NTOK
    lo = max(start, 0)
    xs[lo - start:] = x[b, lo:end]
    valid[lo - start:] = 1.0
    m = {"xs": xs, "mem": np.ascontiguousarray(inputs["mem"][b])}
    m["vmask"] = np.ascontiguousarray(valid.reshape(NCHT, 128).T)
    m["hvalid"] = np.full((128, 1), 1.0 if s > 0 else 0.0, np.float32)
    for name, r, c in W_SPECS:
        m[name] = np.ascontiguousarray(inputs[name][0])
    for name, shape in SMALL_SPECS:
        if name.startswith("c_") or name in ("vmask", "hvalid"):
            continue
        a = inputs[name]
        if name != "final_norm_g":
            a = a[0]
        m[name] = np.ascontiguousarray(a.reshape(shape))
    m.update(make_consts())
    return m


_NC_CACHE = {}


def kernel(**inputs):
    inputs = {k: np.asarray(v) for k, v in inputs.items()}
    if "nc" not in _NC_CACHE:
        _NC_CACHE["nc"] = build_nc()[0]
    nc = _NC_CACHE["nc"]
    in_maps = [core_inputs(inputs, c) for c in range(8)]
    res = run_bass_kernel_spmd(nc, in_maps, core_ids=list(range(8)))
    out = np.zeros((2, 8192, D), np.float32)
    for c in range(8):
        b, s = c // 4, c % 4
        out[b, s * SEG:(s + 1) * SEG] = res.results[c]["out"]
    return out
```

```python
import contextlib
import math
import numpy as np
import concourse.bass as bass
import concourse.mybir as mybir
from concourse.bass_utils import run_bass_kernel_spmd

F32 = mybir.dt.float32
BF16 = mybir.dt.bfloat16
I32 = mybir.dt.int32
AF = mybir.ActivationFunctionType
ALU = mybir.AluOpType
AX = mybir.AxisListType

ENGS = ("tensor", "vector", "scalar", "gpsimd", "sync")
D = 1024
SEG = 2048
NPRE = 6144
NHALO = 128
NTOK = NPRE + NHALO + SEG
NCHT = NTOK // 128
GC = math.sqrt(2.0 / math.pi)


class Buf:
    __slots__ = ("name", "w", "r", "excl")

    def __init__(self, name, excl=False):
        self.name = name
        self.w = None
        self.r = {}
        self.excl = excl


class T:
    def __init__(self, t, name, excl=False):
        self.t = t
        self.b = Buf(name, excl)
        self.subs = None

    def __getitem__(self, k):
        return self.t[k]

    def split(self, keys):
        self.subs = {k: Buf("%s_%s" % (self.b.name, k)) for k in keys}
        return self

    def s(self, *key):
        if len(key) == 1 and key[0] in self.subs:
            return [self.subs[key[0]]]
        return [b for k, b in self.subs.items() if isinstance(k, tuple) and k[:len(key)] == key]


class TV:
    def __init__(self, parent, ap):
        self.t = ap
        self.b = parent.b

    def __getitem__(self, k):
        return self.t[k]


def _bufs(lst):
    out = []
    for x in lst:
        if x is None:
            continue
        if isinstance(x, T) and x.subs:
            out.extend(x.subs.values())
        elif isinstance(x, (T, TV)):
            out.append(x.b)
        elif isinstance(x, (list, tuple)):
            out.extend(_bufs(x))
        else:
            out.append(x)
    return out


class MK:
    def __init__(self, nc, n_dma_ch=16):
        self.nc = nc
        self.ops = {e: [] for e in ENGS}
        self.cnt = {}
        self.known = {e: {} for e in ENGS}
        self.sems = {}
        self._ctx = []
        for e in ENGS:
            self._mk_sem("E_" + e)
        self.n_dma_ch = n_dma_ch
        self.snap = {}
        self.dma_rr = {e: 0 for e in ENGS}
        for e in ("sync", "gpsimd"):
            for c in range(n_dma_ch):
                self._mk_sem("D_%s_%d" % (e, c))

    def _mk_sem(self, key):
        cm = self.nc.semaphore(key)
        h = cm.__enter__()
        self._ctx.append(cm)
        self.sems[key] = h
        self.cnt[key] = 0

    def _need(self, eng, key, val, waits):
        if val <= 0 or self.known[eng].get(key, 0) >= val:
            return
        waits[key] = max(waits.get(key, 0), val)

    def _emit_waits(self, eng, waits):
        kn = self.known[eng]
        for k, v in sorted(waits.items(), key=lambda kv: (kv[0] == "E_tensor", kv[0])):
            if kn.get(k, 0) >= v:
                continue
            kn[k] = v
            h = self.sems[k]
            self.ops[eng].append(lambda E, h=h, v=v: E.wait_ge(h, v))
            s = self.snap.get((k, v))
            if s:
                for kk, vv in s.items():
                    if kn.get(kk, 0) < vv:
                        kn[kk] = vv

    def _deps(self, eng, reads, writes, waits, is_dma=False, xr=()):
        own = "E_" + eng
        for b in reads:
            for k, v in (b.w or {}).items():
                if k == own and not is_dma:
                    if eng == "tensor":
                        continue
                    if id(b) in xr:
                        continue
                    if eng in ("vector", "scalar") and v < self.cnt[own]:
                        continue
                self._need(eng, k, v, waits)
        for b in writes:
            for k, v in (b.w or {}).items():
                if k == own and not is_dma:
                    continue
                if is_dma and k.startswith("D_") and not b.r:
                    continue
                self._need(eng, k, v, waits)
            for k, v in b.r.items():
                if k != own or is_dma:
                    self._need(eng, k, v, waits)

    def _mark(self, key, val, reads, writes, is_dma=False):
        for b in reads:
            if b not in writes:
                b.r[key] = max(b.r.get(key, 0), val)
        for b in writes:
            if is_dma and b.w and not b.r:
                keep = {k: v for k, v in b.w.items() if k.startswith("D_")}
                keep[key] = val
                b.w = keep
            else:
                b.w = {key: val}
            b.r = {}

    def _split(self, reads, writes):
        reads = _bufs(reads)
        writes = _bufs(writes)
        r2 = []
        xr = set()
        for b in reads:
            if b.excl:
                if b not in writes:
                    writes.append(b)
                    xr.add(id(b))
            if b not in r2:
                r2.append(b)
        return r2, writes, xr

    def op(self, eng, fn, reads=(), writes=()):
        reads, writes, xr = self._split(reads, writes)
        waits = {}
        self._deps(eng, reads, writes, waits, xr=xr)
        self._emit_waits(eng, waits)
        key = "E_" + eng
        self.cnt[key] += 1
        val = self.cnt[key]
        h = self.sems[key]
        self.ops[eng].append(lambda E, fn=fn, h=h: fn(E).then_inc(h, 1))
        self.snap[(key, val)] = dict(self.known[eng])
        self._mark(key, val, reads, writes)

    def dma(self, eng, out, in_, reads=(), writes=(), **kw):
        reads, writes, _xr = self._split(reads, writes)
        c = self.dma_rr[eng]
        self.dma_rr[eng] = (c + 1) % self.n_dma_ch
        key = "D_%s_%d" % (eng, c)
        waits = {}
        self._need(eng, key, self.cnt[key], waits)
        self._deps(eng, reads, writes, waits, is_dma=True)
        self._emit_waits(eng, waits)
        self.cnt[key] += 16
        val = self.cnt[key]
        h = self.sems[key]
        self.ops[eng].append(
            lambda E, out=out, in_=in_, h=h, kw=kw: E.dma_start(out=out, in_=in_, **kw).then_inc(h, 16))
        self.snap[(key, val)] = dict(self.known[eng])
        self._mark(key, val, reads, writes, is_dma=True)

    def wait_bufs(self, eng, bufs):
        waits = {}
        for b in _bufs(bufs):
            for k, v in (b.w or {}).items():
                self._need(eng, k, v, waits)
        self._emit_waits(eng, waits)

    def barrier(self):
        for e in ENGS:
            waits = {}
            for k, v in self.cnt.items():
                self._need(e, k, v, waits)
            self._emit_waits(e, waits)

    def emit(self):
        nc = self.nc
        ops = self.ops
        with nc.Block() as block:
            @block.tensor
            def _(E):
                for f in ops["tensor"]:
                    f(E)

            @block.vector
            def _(E):
                for f in ops["vector"]:
                    f(E)

            @block.scalar
            def _(E):
                for f in ops["scalar"]:
                    f(E)

            @block.gpsimd
            def _(E):
                for f in ops["gpsimd"]:
                    f(E)

            @block.sync
            def _(E):
                for f in ops["sync"]:
                    f(E)
        self.ops = {e: [] for e in ENGS}

    def close(self):
        for cm in reversed(self._ctx):
            cm.__exit__(None, None, None)
        self._ctx = []


W_SPECS = [
    ("w_in", 1024, 4616), ("s5_w_glu", 512, 512), ("w_br_s5", 512, 1024), ("w_br_m", 512, 1024),
    ("w_out", 1024, 1024), ("x_wq", 1024, 1024), ("x_wkv", 1024, 2048), ("x_wo", 1024, 1024),
    ("f_w_up", 1024, 5632), ("f_w_down", 2816, 1024),
]
SMALL_SPECS = [
    ("mix_norm_g", [1024]), ("s5_lam_re", [32, 64]), ("s5_lam_im", [32, 64]), ("s5_b_re", [32, 64, 16]),
    ("s5_b_im", [32, 64, 16]), ("s5_c_re", [512, 64]), ("s5_c_im", [512, 64]), ("s5_d", [512]),
    ("s5_log_dt", [32]), ("s5_b_glu", [512]), ("m_conv_w", [4, 1024]), ("m_conv_b", [1024]),
    ("m_b_i", [4]), ("m_b_f", [4]), ("m_norm_g", [512]), ("b_gate", [2048]), ("x_norm_g", [1024]),
    ("mem_norm_g", [1024]), ("f_norm_g", [1024]), ("f_conv_w", [3, 5632]), ("f_conv_b", [5632]),
    ("final_norm_g", [1024]),
    ("c_ident", [128, 128]), ("c_tri", [128, 128]), ("c_ones", [128, 128]), ("c_mmask", [128, 128]),
    ("vmask", [128, NCHT]), ("hvalid", [128, 1]),
]


class KB:
    def __init__(self, nc, es, dbg=None):
        self.nc = nc
        self.es = es
        self.mk = MK(nc)
        self.dbg = dbg
        self.din = {}
        self.banks = []
        self.tbanks = []
        self.bank_i = 0
        self.tbank_i = 0
        self.tmpf = []
        self.tmpf_i = 0
        self.tmpb = []
        self.tmpb_i = 0
        self.wslots = []
        self.wslot_i = 0
        self.bq = "sync"

    def sb(self, name, shape, dt, es=None):
        t = (es or self.es).enter_context(self.nc.sbuf_tensor(name, shape, dt))
        return T(t, name)

    def dram(self, name, shape, dt, kind):
        return T(self.nc.dram_tensor(name, shape, dt, kind=kind).ap(), name)

    def bank(self):
        b = self.banks[self.bank_i % len(self.banks)]
        self.bank_i += 1
        return b

    def tbank(self):
        b = self.tbanks[self.tbank_i % len(self.tbanks)]
        self.tbank_i += 1
        return b

    def tf(self):
        b = self.tmpf[self.tmpf_i % len(self.tmpf)]
        self.tmpf_i += 1
        return b

    def tb(self):
        b = self.tmpb[self.tmpb_i % len(self.tmpb)]
        self.tmpb_i += 1
        return b

    def V(self, fn, r=(), w=()):
        self.mk.op("vector", fn, r, w)

    def A(self, fn, r=(), w=()):
        self.mk.op("scalar", fn, r, w)

    def G(self, fn, r=(), w=()):
        self.mk.op("gpsimd", fn, r, w)

    def P(self, fn, r=(), w=()):
        self.mk.op("tensor", fn, r, w)

    def act(self, out, in_, func, r, w, **kw):
        self.A(lambda E: E.activation(out=out, in_=in_, func=func, **kw), r, w)

    def tt(self, eng, out, in0, in1, op, r, w):
        self.mk.op(eng, lambda E: E.tensor_tensor(out=out, in0=in0, in1=in1, op=op), r, w)

    def ts(self, eng, out, in0, s1, s2, op0, op1, r, w):
        if s2 is None:
            self.mk.op(eng, lambda E: E.tensor_scalar(out=out, in0=in0, scalar1=s1, scalar2=None, op0=op0), r, w)
        else:
            self.mk.op(eng, lambda E: E.tensor_scalar(out=out, in0=in0, scalar1=s1, scalar2=s2, op0=op0, op1=op1), r, w)

    def stt(self, out, in0, sc, in1, op0, op1, r, w):
        self.V(lambda E: E.scalar_tensor_tensor(out=out, in0=in0, scalar=sc, in1=in1, op0=op0, op1=op1), r, w)

    def cp(self, eng, out, in_, r, w):
        if eng == "scalar":
            self.A(lambda E: E.copy(out=out, in_=in_), r, w)
        else:
            self.mk.op(eng, lambda E: E.tensor_copy(out=out, in_=in_), r, w)

    def mm(self, out, lhsT, rhs, start, stop, r, w):
        self.P(lambda E: E.matmul(out, lhsT=lhsT, rhs=rhs, start=start, stop=stop), r, w)

    def tr(self, out, in_, ident, r, w):
        self.P(lambda E: E.transpose(out=out, in_=in_, identity=ident), r, w)

    def dma(self, eng, out, in_, r, w, **kw):
        self.mk.dma(eng, out, in_, r, w, **kw)

    def memset(self, eng, ap, val, w):
        self.mk.op(eng, lambda E: E.memset(ap, val), (), w)

    def wslab(self, src_t, src_ap, kt, ncols, half=None, slot=None):
        ph = self.wphase
        if slot is not None:
            idx = slot
            self._last_slot = slot
        elif half is None or half == 0:
            idx = self.wpool[ph][self.wpool_i[ph] % len(self.wpool[ph])]
            self.wpool_i[ph] += 1
            self._last_slot = idx
        else:
            idx = self._last_slot
        s = self.wslots[idx]
        off = 0 if not half else 2048
        view = s.t[:, off:off + kt * ncols].rearrange("p (k n) -> p k n", k=kt)
        self.dma("sync", view, src_ap, [src_t], [s])
        return s, view

    def declare_io(self):
        nc = self.nc
        self.xs = self.dram("xs", [NTOK, D], F32, "ExternalInput")
        self.mem = self.dram("mem", [256, D], F32, "ExternalInput")
        self.out = self.dram("out", [SEG, D], F32, "ExternalOutput")
        self.wf = {}
        self.wb = {}
        for name, r, c in W_SPECS:
            self.wf[name] = self.dram(name, [r, c], F32, "ExternalInput")
            self.wb[name] = self.dram(name + "_bf", [r, c], BF16, "Internal")
        for name, shape in SMALL_SPECS:
            self.din[name] = self.dram(name, shape, F32, "ExternalInput")
        self.scrA = self.dram("scrA", [512, 8, 64], BF16, "Internal")
        self.scrB = self.dram("scrB", [512, 8, 64], BF16, "Internal")

    def precast_cols(self, name, blocks, defer=False):
        lst = self.wblocks.setdefault(name, [])
        for (c0, c1) in blocks:
            bb = Buf("%s_%d" % (name, c0))
            lst.append((c0, c1, bb))
            if defer:
                self.pc_jobs.append((name, c0, c1, bb))
            else:
                self.dma("gpsimd", self.wb[name][:, c0:c1], self.wf[name][:, c0:c1], [], [bb])

    def precast_flush(self, n, after=()):
        for _ in range(n):
            if not self.pc_jobs:
                return
            name, c0, c1, bb = self.pc_jobs.pop(0)
            self.dma("gpsimd", self.wb[name][:, c0:c1], self.wf[name][:, c0:c1], list(after), [bb])

    def wdep(self, name, c0=0, c1=1 << 30):
        return [bb for (a0, a1, bb) in self.wblocks[name] if a0 < c1 and c0 < a1]

    def alloc_core(self):
        sb = self.sb
        for i in range(6):
            t = self.es.enter_context(self.nc.psum_tensor("bank%d" % i, [128, 512], F32))
            self.banks.append(T(t, "bank%d" % i, excl=True))
        for i in range(2):
            t = self.es.enter_context(self.nc.psum_tensor("tbank%d" % i, [128, 1024], BF16))
            self.tbanks.append(T(t, "tbank%d" % i, excl=True))
        self.identf = sb("identf", [128, 128], F32)
        self.identb = sb("identb", [128, 128], BF16)
        self.trif = sb("trif", [128, 128], F32)
        self.onesf = sb("onesf", [128, 128], F32)
        self.onesb = sb("onesb", [128, 128], BF16)
        self.consts = sb("consts", [128, 8], F32)
        self.Mw = sb("Mw", [128, 32, 128], BF16)
        self.WsRe = sb("WsRe", [128, 32, 64], BF16)
        self.WsIm = sb("WsIm", [128, 32, 64], BF16)
        self.VwRe = sb("VwRe", [128, 16, 128], BF16)
        self.VwIm = sb("VwIm", [128, 16, 128], BF16)
        self.cpr = sb("cpr", [128, 7, 16], F32)
        self.cpi = sb("cpi", [128, 7, 16], F32)

    def set_par(self, p):
        self.xt, self.hT, self.ss, self.rstd = self.xtb[p], self.hTb[p], self.ssb[p], self.rstdb[p]

    def set_par2(self, q):
        a = self.alt[q]
        self.Uall, self.vext, self.wk2, self.ebg, self.kbase = a["Uall"], a["vext"], a["wk2"], a["ebg"], a["kbase"]

    def alloc_work(self):
        sb = self.sb
        for i in range(9):
            self.tmpf.append(sb("tmpf%d" % i, [128, 516], F32))
        for i in range(3):
            self.tmpb.append(sb("tmpb%d" % i, [128, 1024], BF16))
        for i in range(4):
            self.wslots.append(sb("wslot%d" % i, [128, 4096], BF16))
        self.wpool = {"front": [0, 1], "back": [2, 3]}
        self.wpool_i = {"front": 0, "back": 0}
        self.wphase = "front"
        self.g1T = sb("g1T", [128, 8], F32)
        self.g2T = sb("g2T", [128, 8], F32)
        self.g3T = sb("g3T", [128, 8], F32)
        self.gmT = sb("gmT", [128, 8], F32)
        self.cw = sb("cw", [128, 8, 4], F32)
        self.cb = sb("cb", [128, 8], F32)
        self.bif = sb("bif", [128, 8], F32)
        self.mng = sb("mng", [128, 512], F32)
        self.bgT = sb("bgT", [128, 16], F32)
        self.bgluT = sb("bgluT", [128, 4], F32)
        self.fcw = sb("fcw", [128, 44, 3], F32)
        self.fcb = sb("fcb", [128, 44], F32)
        self.vmask = sb("vmask_s", [128, NCHT], F32)
        self.hvalid = sb("hvalid_s", [128, 1], F32)
        self.ifslab = sb("ifslab", [128, 8, 8], BF16)
        self.carry = sb("carry", [128, 16, 2], F32)
        self.Cst = sb("Cst", [128, 4, 129], F32)
        self.Cbf = sb("Cbf", [128, 4, 129], BF16)
        self.qkh = sb("qkh", [128, 8, 3], F32).split([0, 1])
        self.uprev = sb("uprev", [128, 44, 2], F32).split([0, 1])
        self.kxT = sb("kxT", [128, 8, 256], BF16)
        self.vx = sb("vx", [128, 2, 1024], BF16)
        self.xtb = [sb("xt%d" % i, [128, 4, 1024], F32).split([(c, h) for c in range(4) for h in range(2)]) for i in range(2)]
        self.ssb = [sb("ss%d" % i, [128, 4], F32) for i in range(2)]
        self.rstdb = [sb("rstd%d" % i, [128, 4], F32) for i in range(2)]
        self.hTb = [sb("hT%d" % i, [128, 8, 512], BF16) for i in range(2)]
        self.set_par(0)
        self.usT = sb("usT", [128, 4, 8, 64], BF16)
        self.Uall = sb("Uall", [128, 32, 64], BF16)
        self.X0 = sb("X0", [128, 16, 2, 64], F32).split([(h, e) for h in range(2) for e in range(2)])
        self.Xp = sb("Xp", [128, 16, 2, 64], BF16)
        self.stA = sb("stA", [128, 1024], F32).split(["re", "im"])
        self.stB = sb("stB", [128, 1024], F32).split(["re", "im"])
        self.ys5T = sb("ys5T", [128, 4, 512], BF16)
        self.qkT = sb("qkT", [128, 8, 512], BF16)
        self.vext = sb("vext", [128, 4, 4, 129], BF16)
        self.Ygl = TV(self.vext, self.vext[:, :, :, :].rearrange("p a b c -> p (a b c)")[:, 0:2048].rearrange("p (g j) -> p g j", g=32))
        self.sgo = TV(self.ys5T, self.ys5T[:, :, :])
        self.gif = sb("gif", [128, 4, 8], F32)
        self.l1 = sb("l1", [128, 4, 4], F32)
        self.l1b = sb("l1b", [128, 4, 4], F32)
        self.nbg = sb("nbg", [128, 4, 8], F32)
        self.wk = sb("wk", [128, 4, 4], F32)
        self.wk2 = sb("wk2", [128, 4, 4], F32)
        self.sfx = sb("sfx", [128, 4, 4], F32)
        self.wkf = sb("wkf", [128, 4, 4], F32)
        self.ebg = sb("ebg", [128, 4, 8], F32)
        self.sm = sb("sm", [128, 16], F32)
        self.hss = sb("hss", [128, 12], F32)
        self.junk2 = sb("junk2", [128, 128], BF16)
        self.ymT = TV(self.usT, self.usT[:, :, :, :].rearrange("p a s j -> p a (s j)"))
        self.mergedT = sb("mergedT", [128, 8, 512], BF16)
        self.gT = sb("gT", [128, 6, 512], BF16)
        Uall_alt = TV(self.gT, self.gT[:, :, :].rearrange("p a b -> p (a b)")[:, 0:2048].rearrange("p (g j) -> p g j", g=32))
        vext_alt = TV(self.mergedT, self.mergedT[:, :, :].rearrange("p a b -> p (a b)")[:, 0:2064].rearrange("p (c h e) -> p c h e", c=4, h=4))
        self.alt = [
            {"Uall": self.Uall, "vext": self.vext, "wk2": self.wk2, "ebg": self.ebg, "kbase": 4},
            {"Uall": Uall_alt, "vext": vext_alt, "wk2": sb("wk2b", [128, 4, 4], F32), "ebg": sb("ebgb", [128, 4, 8], F32), "kbase": 0},
        ]
        self.kbase = 4
        self.hg = sb("hg", [128, 512], F32)

    def load_consts(self):
        d = self.din
        dm = self.dma
        dm("gpsimd", self.identb[:], d["c_ident"][:, :], [], [self.identb])
        dm("sync", self.trif[:], d["c_tri"][:, :], [], [self.trif])
        dm("sync", self.onesf[:], d["c_ones"][:, :], [], [self.onesf])
        dm("gpsimd", self.onesb[:], d["c_ones"][:, :], [], [self.onesb])
        for t, nm in ((self.g1T, "mix_norm_g"), (self.g2T, "x_norm_g"), (self.g3T, "f_norm_g"), (self.gmT, "mem_norm_g")):
            dm("sync", t[:], d[nm].t.rearrange("(k p) -> p k", p=128), [], [t], allow_slow_non_contiguous=True)
        for k in range(4):
            dm("sync", self.cw[:, :, k], d["m_conv_w"].t[k, :].rearrange("(m p) -> p m", p=128), [], [self.cw], allow_slow_non_contiguous=True)
        dm("sync", self.cb[:], d["m_conv_b"].t.rearrange("(m p) -> p m", p=128), [], [self.cb], allow_slow_non_contiguous=True)
        dm("sync", self.bif[:, 0:4], d["m_b_i"].t.partition_broadcast(128), [], [self.bif])
        dm("sync", self.bif[:, 4:8], d["m_b_f"].t.partition_broadcast(128), [], [self.bif])
        dm("sync", self.mng[:], d["m_norm_g"].t.partition_broadcast(128), [], [self.mng])
        dm("sync", self.bgT[:], d["b_gate"].t.rearrange("(m p) -> p m", p=128), [], [self.bgT], allow_slow_non_contiguous=True)
        dm("sync", self.bgluT[:], d["s5_b_glu"].t.rearrange("(m p) -> p m", p=128), [], [self.bgluT], allow_slow_non_contiguous=True)
        for k in range(3):
            dm("sync", self.fcw[:, :, k], d["f_conv_w"].t[k, :].rearrange("(m p) -> p m", p=128), [], [self.fcw], allow_slow_non_contiguous=True)
        dm("sync", self.fcb[:], d["f_conv_b"].t.rearrange("(m p) -> p m", p=128), [], [self.fcb], allow_slow_non_contiguous=True)
        dm("sync", self.vmask[:], d["vmask"][:, :], [], [self.vmask])
        dm("sync", self.hvalid[:], d["hvalid"][:, :], [], [self.hvalid])
        for t_, ap_ in ((self.cw, self.cw[:]), (self.cb, self.cb[:]), (self.fcw, self.fcw[:, 0:22, :]),
                        (self.fcb, self.fcb[:, 0:22]), (self.bgT, self.bgT[:]), (self.bgluT, self.bgluT[:])):
            self.act(ap_, ap_, AF.Copy, [t_], [t_], scale=0.5)
        for t in (self.carry, self.Cst, self.Cbf, self.qkh, self.uprev):
            self.memset("gpsimd", t[:], 0.0, [t])
        self.memset("gpsimd", self.vext[:], 1.0, [self.vext])
        self.memset("gpsimd", self.alt[1]["vext"][:, :, :, :], 1.0, [self.alt[1]["vext"]])

    def s5_setup(self):
        with contextlib.ExitStack() as es2:
            sb = lambda n, s, dt=F32: self.sb("s5s_" + n, s, dt, es2)
            d = self.din
            V = self.V
            lamr, lami = sb("lamr", [128, 32]), sb("lami", [128, 32])
            dtb = sb("dtb", [128, 32])
            Br, Bi = sb("Br", [128, 32, 16]), sb("Bi", [128, 32, 16])
            cin_r, cin_i = sb("cinr", [128, 4, 2, 64]), sb("cini", [128, 4, 2, 64])
            Cr, Ci = sb("Cr", [128, 32, 16]), sb("Ci", [128, 32, 16])
            dcol = sb("dcol", [128, 32])
            mmask = sb("mmask", [128, 128])
            for h0 in (0, 64):
                self.dma("sync", lamr[h0:h0 + 64, :], d["s5_lam_re"].t.rearrange("g p -> p g"), [], [lamr], allow_slow_non_contiguous=True)
                self.dma("sync", lami[h0:h0 + 64, :], d["s5_lam_im"].t.rearrange("g p -> p g"), [], [lami], allow_slow_non_contiguous=True)
                self.dma("sync", Br[h0:h0 + 64, :, :], d["s5_b_re"].t.rearrange("g p h -> p g h"), [], [Br])
                self.dma("sync", Bi[h0:h0 + 64, :, :], d["s5_b_im"].t.rearrange("g p h -> p g h"), [], [Bi])
            self.dma("sync", dtb[:], d["s5_log_dt"].t.partition_broadcast(128), [], [dtb])
            for e in (0, 1):
                self.dma("sync", cin_r[:, :, e, :], d["s5_c_re"].t.rearrange("(a q) p -> q a p", q=128), [], [cin_r])
                self.dma("sync", cin_i[:, :, e, :], d["s5_c_im"].t.rearrange("(a q) p -> q a p", q=128), [], [cin_i])
            for s in range(8):
                self.dma("sync", dcol[s * 16:(s + 1) * 16, :], d["s5_d"].t.rearrange("(g i) -> i g", i=16), [], [dcol], allow_slow_non_contiguous=True)
            self.dma("sync", mmask[:], d["c_mmask"][:, :], [], [mmask])
            for src, dst in ((cin_r, Cr), (cin_i, Ci)):
                for a in range(4):
                    pb = self.bank()
                    self.tr(pb[:, 0:128], src[:, a, :, :], self.identf[:], [src, self.identf], [pb])
                    self.cp("vector", dst[:, a * 8:(a + 1) * 8, :], pb[:, 0:128].rearrange("p (g o) -> p g o", o=16), [pb], [dst])
            self.act(dtb[:], dtb[:], AF.Exp, [dtb], [dtb])
            rho, th = sb("rho", [128, 32]), sb("th", [128, 32])
            self.tt("vector", rho[:], lamr[:], dtb[:], ALU.mult, [lamr, dtb], [rho])
            self.act(rho[:], rho[:], AF.Exp, [rho], [rho])
            self.tt("vector", th[:], lami[:], dtb[:], ALU.mult, [lami, dtb], [th])
            ki, kf = sb("ki", [128, 32], I32), sb("kf", [128, 32])
            self.ts("vector", kf[:], th[:], 1.0 / (2 * math.pi), None, ALU.mult, None, [th], [kf])
            self.cp("vector", ki[:], kf[:], [kf], [ki])
            self.cp("vector", kf[:], ki[:], [ki], [kf])
            self.stt(th[:], kf[:], -2 * math.pi, th[:], ALU.mult, ALU.add, [kf, th], [th])
            s4, c4 = sb("s4", [128, 32]), sb("c4", [128, 32])
            self.act(s4[:], th[:], AF.Sin, [th], [s4], scale=0.25)
            self.act(c4[:], th[:], AF.Sin, [th, self.consts], [c4], scale=0.25, bias=self.consts[:, 1:2])
            tA, tB, tC = sb("tA", [128, 32]), sb("tB", [128, 32]), sb("tC", [128, 32])

            def csq(cr, ci):
                self.tt("vector", tA[:], cr[:], cr[:], ALU.mult, [cr], [tA])
                self.tt("vector", tB[:], ci[:], ci[:], ALU.mult, [ci], [tB])
                self.tt("vector", tC[:], cr[:], ci[:], ALU.mult, [cr, ci], [tC])
                self.tt("vector", cr[:], tA[:], tB[:], ALU.subtract, [tA, tB], [cr])
                self.ts("vector", ci[:], tC[:], 2.0, None, ALU.mult, None, [tC], [ci])
            csq(c4, s4)
            csq(c4, s4)
            ar, ai = sb("ar", [128, 32]), sb("ai", [128, 32])
            self.tt("vector", ar[:], rho[:], c4[:], ALU.mult, [rho, c4], [ar])
            self.tt("vector", ai[:], rho[:], s4[:], ALU.mult, [rho, s4], [ai])

            def cmul(orr, oi, xr, xi, yr, yi, shp=None):
                (orr_a, orr_t), (oi_a, oi_t) = orr, oi
                (xr_a, xr_t), (xi_a, xi_t), (yr_a, yr_t), (yi_a, yi_t) = xr, xi, yr, yi
                u1, u2 = sb_tmp(shp)
                self.tt("vector", u1[0], xr_a, yr_a, ALU.mult, [xr_t, yr_t], [u1[1]])
                self.tt("vector", u2[0], xi_a, yi_a, ALU.mult, [xi_t, yi_t], [u2[1]])
                u3, u4 = sb_tmp(shp)
                self.tt("vector", u3[0], xr_a, yi_a, ALU.mult, [xr_t, yi_t], [u3[1]])
                self.tt("vector", u4[0], xi_a, yr_a, ALU.mult, [xi_t, yr_t], [u4[1]])
                self.tt("vector", orr_a, u1[0], u2[0], ALU.subtract, [u1[1], u2[1]], [orr_t])
                self.tt("vector", oi_a, u3[0], u4[0], ALU.add, [u3[1], u4[1]], [oi_t])

            tmp_pool = [sb("cm%d" % i, [128, 8 * 8 * 16]) for i in range(8)]
            tmp_i = [0]

            def sb_tmp(shp):
                res = []
                for _ in range(2):
                    t = tmp_pool[tmp_i[0] % 8]
                    tmp_i[0] += 1
                    n = int(np.prod(shp[1:]))
                    ap = t[:, 0:n]
                    if len(shp) == 3:
                        ap = ap.rearrange("p (a b) -> p a b", a=shp[1])
                    elif len(shp) == 4:
                        ap = ap.rearrange("p (a b c) -> p a b c", a=shp[1], b=shp[2])
                    res.append((ap, t))
                return res

            apr, api = sb("apr", [128, 9, 32]), sb("api", [128, 9, 32])
            self.memset("vector", apr[:, 0, :], 1.0, [apr])
            self.memset("vector", api[:, 0, :], 0.0, [api])
            self.cp("vector", apr[:, 1, :], ar[:], [ar], [apr])
            self.cp("vector", api[:, 1, :], ai[:], [ai], [api])
            for k in range(2, 9):
                cmul((apr[:, k, :], apr), (api[:, k, :], api), (apr[:, k - 1, :], apr), (api[:, k - 1, :], api),
                     (ar[:], ar), (ai[:], ai), [128, 32])
            n2, ivr, ivi = sb("n2", [128, 32]), sb("ivr", [128, 32]), sb("ivi", [128, 32])
            self.tt("vector", tA[:], ar[:], ar[:], ALU.mult, [ar], [tA])
            self.tt("vector", tB[:], ai[:], ai[:], ALU.mult, [ai], [tB])
            self.tt("vector", n2[:], tA[:], tB[:], ALU.add, [tA, tB], [n2])
            self.V(lambda E: E.reciprocal(out=n2[:], in_=n2[:]), [n2], [n2])
            self.tt("vector", ivr[:], ar[:], n2[:], ALU.mult, [ar, n2], [ivr])
            self.stt(ivi[:], ai[:], -1.0, n2[:], ALU.mult, ALU.mult, [ai, n2], [ivi])
            ipr, ipi = sb("ipr", [128, 8, 32]), sb("ipi", [128, 8, 32])
            self.memset("vector", ipr[:, 0, :], 1.0, [ipr])
            self.memset("vector", ipi[:, 0, :], 0.0, [ipi])
            self.cp("vector", ipr[:, 1, :], ivr[:], [ivr], [ipr])
            self.cp("vector", ipi[:, 1, :], ivi[:], [ivi], [ipi])
            for k in range(2, 8):
                cmul((ipr[:, k, :], ipr), (ipi[:, k, :], ipi), (ipr[:, k - 1, :], ipr), (ipi[:, k - 1, :], ipi),
                     (ivr[:], ivr), (ivi[:], ivi), [128, 32])
            for nm_, t_ in (("ar", ar), ("ai", ai), ("ivr", ivr), ("ivi", ivi)):
                self.dump(nm_, t_, t_[:, :])
            for nm_, t_ in (("ipr", ipr), ("ipi", ipi), ("apr", apr), ("api", api)):
                self.dump(nm_, t_, t_[:, :, :])
            dr, di = sb("dr", [128, 7, 32]), sb("di", [128, 7, 32])
            self.cp("vector", dr[:, 0, :], apr[:, 8, :], [apr], [dr])
            self.cp("vector", di[:, 0, :], api[:, 8, :], [api], [di])
            for k in range(1, 7):
                cmul((dr[:, k, :], dr), (di[:, k, :], di), (dr[:, k - 1, :], dr), (di[:, k - 1, :], di),
                     (dr[:, k - 1, :], dr), (di[:, k - 1, :], di), [128, 32])
            for (src, dst) in ((dr, self.cpr), (di, self.cpi)):
                v = src[:, :, :].rearrange("p k (a e) -> p k a e", e=2)
                self.cp("vector", dst[0:64, :, :], v[0:64, :, :, 0], [src], [dst])
                self.cp("vector", dst[64:128, :, :], v[64:128, :, :, 1], [src], [dst])
            qr, qi = sb("qr", [128, 32]), sb("qi", [128, 32])
            am1 = sb("am1", [128, 32])
            self.ts("vector", am1[:], ar[:], -1.0, None, ALU.add, None, [ar], [am1])
            self.tt("vector", tA[:], lamr[:], lamr[:], ALU.mult, [lamr], [tA])
            self.tt("vector", tB[:], lami[:], lami[:], ALU.mult, [lami], [tB])
            self.tt("vector", n2[:], tA[:], tB[:], ALU.add, [tA, tB], [n2])
            self.V(lambda E: E.reciprocal(out=n2[:], in_=n2[:]), [n2], [n2])
            self.tt("vector", tA[:], am1[:], lamr[:], ALU.mult, [am1, lamr], [tA])
            self.tt("vector", tB[:], ai[:], lami[:], ALU.mult, [ai, lami], [tB])
            self.tt("vector", qr[:], tA[:], tB[:], ALU.add, [tA, tB], [qr])
            self.tt("vector", tA[:], ai[:], lamr[:], ALU.mult, [ai, lamr], [tA])
            self.tt("vector", tB[:], am1[:], lami[:], ALU.mult, [am1, lami], [tB])
            self.tt("vector", qi[:], tA[:], tB[:], ALU.subtract, [tA, tB], [qi])
            self.tt("vector", qr[:], qr[:], n2[:], ALU.mult, [qr, n2], [qr])
            self.tt("vector", qi[:], qi[:], n2[:], ALU.mult, [qi, n2], [qi])
            Bbr, Bbi = sb("Bbr", [128, 32, 16]), sb("Bbi", [128, 32, 16])
            bc = lambda t: t[:, :].unsqueeze(2).to_broadcast([128, 32, 16])
            cmul((Bbr[:], Bbr), (Bbi[:], Bbi), (bc(qr), qr), (bc(qi), qi), (Br[:], Br), (Bi[:], Bi), [128, 32, 16])

            Wr, Wi = sb("Wr", [128, 8, 8, 16]), sb("Wi", [128, 8, 8, 16])
            Vr, Vi = sb("Vr", [128, 8, 8, 16]), sb("Vi", [128, 8, 8, 16])
            Wstk, Vstk = sb("Wstk", [128, 8, 128]), sb("Vstk", [128, 8, 128])
            mtmp = sb("mtmp", [128, 128])

            def pw_b(src, k, gb):
                return src[:, k, gb * 8:(gb + 1) * 8]

            aRr, aRi = sb("aRr", [128, 8, 32]), sb("aRi", [128, 8, 32])
            for s in range(8):
                self.cp("vector", aRr[:, s, :], apr[:, 7 - s, :], [apr], [aRr])
                self.cp("vector", aRi[:, s, :], api[:, 7 - s, :], [api], [aRi])

            def pw4(t, k0, gs):
                return t[:, k0:k0 + 8, gs].rearrange("p k g -> p g k").unsqueeze(3).to_broadcast([128, 8, 8, 16])

            def b4(t, gs):
                return t[:, gs, :].unsqueeze(2).to_broadcast([128, 8, 8, 16])

            for gb in range(4):
                gs = slice(gb * 8, (gb + 1) * 8)
                cmul((Wr[:], Wr), (Wi[:], Wi), (pw4(ipr, 0, gs), ipr), (pw4(ipi, 0, gs), ipi),
                     (b4(Bbr, gs), Bbr), (b4(Bbi, gs), Bbi), [128, 8, 8, 16])
                cmul((Vr[:], Vr), (Vi[:], Vi), (pw4(apr, 0, gs), apr), (pw4(api, 0, gs), api),
                     (b4(Cr, gs), Cr), (b4(Ci, gs), Ci), [128, 8, 8, 16])
                fl = lambda t: t[:, :, :, :].rearrange("p g s i -> p g (s i)")
                self.cp("vector", Wstk[0:64, :, :], fl(Wr)[0:64], [Wr], [Wstk])
                self.cp("vector", Wstk[64:128, :, :], fl(Wi)[64:128], [Wi], [Wstk])
                self.cp("vector", Vstk[0:64, :, :], fl(Vr)[0:64], [Vr], [Vstk])
                self.ts("vector", Vstk[64:128, :, :], fl(Vi)[64:128], -1.0, None, ALU.mult, None, [Vi], [Vstk])
                for gl in range(8):
                    g = gb * 8 + gl
                    pb = self.bank()
                    self.mm(pb[:, 0:128], Wstk[:, gl, :], Vstk[:, gl, :], True, True, [Wstk, Vstk], [pb])
                    self.tt("vector", mtmp[:], pb[:, 0:128], mmask[:], ALU.mult, [pb, mmask], [mtmp])
                    self.stt(self.Mw[:, g, :], self.identf[:], dcol[:, g:g + 1], mtmp[:], ALU.mult, ALU.add,
                             [self.identf, dcol, mtmp], [self.Mw])
                cmul((Wr[:], Wr), (Wi[:], Wi), (pw4(aRr, 0, gs), aRr), (pw4(aRi, 0, gs), aRi),
                     (b4(Bbr, gs), Bbr), (b4(Bbi, gs), Bbi), [128, 8, 8, 16])
                for (src, dst) in ((Wr, self.WsRe), (Wi, self.WsIm)):
                    for gl in range(8):
                        g = gb * 8 + gl
                        pb = self.bank()
                        self.tr(pb[:, 0:64], fl(src)[0:64, gl, :], self.identf[0:64, 0:64], [src, self.identf], [pb])
                        self.cp("vector", dst[:, g, :], pb[:, 0:64], [pb], [dst])
                cmul((Vr[:], Vr), (Vi[:], Vi), (pw4(apr, 1, gs), apr), (pw4(api, 1, gs), api),
                     (b4(Cr, gs), Cr), (b4(Ci, gs), Ci), [128, 8, 8, 16])
                vre = fl(Vr).rearrange("p (a e) f -> p a e f", e=2)
                vim = fl(Vi).rearrange("p (a e) f -> p a e f", e=2)
                ps_ = slice(gb * 4, (gb + 1) * 4)
                self.cp("vector", self.VwRe[0:64, ps_, :], vre[0:64, :, 0, :], [Vr], [self.VwRe])
                self.cp("vector", self.VwRe[64:128, ps_, :], vre[64:128, :, 1, :], [Vr], [self.VwRe])
                self.ts("vector", self.VwIm[0:64, ps_, :], vim[0:64, :, 0, :], -1.0, None, ALU.mult, None, [Vi], [self.VwIm])
                self.ts("vector", self.VwIm[64:128, ps_, :], vim[64:128, :, 1, :], -1.0, None, ALU.mult, None, [Vi], [self.VwIm])
            self.mk.barrier()
            self.mk.emit()

    def w3(self, name, c0, c1):
        return self.wb[name].t.rearrange("(k p) n -> p k n", p=128)[:, :, c0:c1]

    def norm_T(self, nch, gT):
        xt, ss, rstd, hT = self.xt, self.ss, self.rstd, self.hT
        self.memset("vector", ss[:], 0.0, [ss])
        for c in range(nch):
            junk = self.tb()
            self.act(junk[:], xt[:, c, :], AF.Square, [xt.s(c)], [junk, ss], accum_out=ss[:, c:c + 1])
        self.ts("vector", rstd[:, 0:nch], ss[:, 0:nch], 1.0 / D, 1e-6, ALU.mult, ALU.add, [ss], [rstd])
        self.tt("gpsimd", rstd[:, 0:nch], rstd[:, 0:nch], self.consts[:, 2:3].to_broadcast([128, nch]), ALU.pow,
                [rstd, self.consts], [rstd])
        for c in range(nch):
            xn = self.tb()
            self.act(xn[:], xt[:, c, :], AF.Copy, [xt.s(c), rstd], [xn], scale=rstd[:, c:c + 1])
            tbk = self.tbank()
            for k in range(8):
                self.tr(tbk[:, k * 128:(k + 1) * 128], xn[:, k * 128:(k + 1) * 128], self.identb[:], [xn, self.identb], [tbk])
            self.tt("vector", hT[:, :, c * 128:(c + 1) * 128], tbk[:, :].rearrange("p (k t) -> p k t", k=8),
                    gT[:, :].unsqueeze(2).to_broadcast([128, 8, 128]), ALU.mult, [tbk, gT], [hT])

    def fm_matmul(self, pb, wv, c0, src, kt, N, rd):
        for k in range(kt):
            self.mm(pb[:, 0:N], wv[:, k, c0:c0 + 128], src[:, k, 0:N], k == 0, k == kt - 1, rd, [pb])

    def tm_matmul(self, pb, src, c, wv, c0, ncols, kt, rd):
        for k in range(kt):
            self.mm(pb[:, 0:ncols], src[:, k, c * 128:(c + 1) * 128], wv[:, k, c0:c0 + ncols], k == 0, k == kt - 1, rd, [pb])

    def conv_silu2(self, items, N):
        tmp = [(self.tf(), self.tf()) for _ in items]
        for (pb, mt, dst), (raw, acc) in zip(items, tmp):
            hs = self.qkh.s(mt % 2)
            self.cp("scalar", raw[:, 0:3], self.qkh[:, mt, :], [hs], [raw])
            self.cp("scalar", raw[:, 3:3 + N], pb[:, 0:N], [pb], [raw])
            self.act(acc[:, 0:N], pb[:, 0:N], AF.Identity, [pb, self.cw, self.cb], [acc],
                     scale=self.cw[:, mt, 3:4], bias=self.cb[:, mt:mt + 1])
            self.cp("scalar", self.qkh[:, mt, :], raw[:, N:N + 3], [raw], [hs])
        for k in (2, 1, 0):
            for (pb, mt, dst), (raw, acc) in zip(items, tmp):
                self.stt(acc[:, 0:N], raw[:, k:k + N], self.cw[:, mt, k:k + 1], acc[:, 0:N], ALU.mult, ALU.add,
                         [raw, self.cw, acc], [acc])
        for (pb, mt, dst), (raw, acc) in zip(items, tmp):
            self.act(raw[:, 0:N], acc[:, 0:N], AF.Tanh, [acc], [raw])
        for (pb, mt, dst), (raw, acc) in zip(items, tmp):
            self.stt(self.qkT[:, dst, 0:N], raw[:, 0:N], 1.0, acc[:, 0:N], ALU.add, ALU.mult, [raw, acc], [self.qkT])

    def gates_all(self, nch, cg0, fold=False):
        gif, l1, nbg, wk, ebg = self.gif, self.l1, self.nbg, self.wk, self.ebg
        pb = self.bank()
        for c in range(nch):
            for k in range(8):
                self.mm(pb[:, c * 8:(c + 1) * 8], self.hT[:, k, c * 128:(c + 1) * 128], self.ifslab[:, k, 0:8], k == 0, k == 7,
                        [self.hT, self.ifslab], [pb])
        self.tt("vector", gif[:, 0:nch, :], pb[:, 0:nch * 8].rearrange("p (c e) -> p c e", e=8),
                self.bif[:, :].unsqueeze(1).to_broadcast([128, nch, 8]), ALU.add, [pb, self.bif], [gif])
        self.act(l1[:, 0:nch, :], gif[:, 0:nch, 4:8], AF.Exp, [gif], [l1], scale=-1.0)
        self.act(self.l1b[:, 0:nch, :], l1[:, 0:nch, :], AF.Ln, [l1, self.consts], [self.l1b], bias=self.consts[:, 0:1])
        pb2 = self.bank()
        for c in range(nch):
            self.mm(pb2[:, c * 8:c * 8 + 4], self.trif[:], self.l1b[:, c, :], True, True, [self.trif, self.l1b], [pb2])
            self.mm(pb2[:, c * 8 + 4:c * 8 + 8], self.onesf[:], self.l1b[:, c, :], True, True, [self.onesf, self.l1b], [pb2])
        self.cp("vector", nbg[:, 0:nch, :], pb2[:, 0:nch * 8].rearrange("p (c e) -> p c e", e=8), [pb2], [nbg])
        self.tt("vector", wk[:, 0:nch, :], gif[:, 0:nch, 0:4], nbg[:, 0:nch, 0:4], ALU.add, [gif, nbg], [wk])
        if fold:
            sfx = self.sfx
            self.cp("vector", sfx[:, nch - 1, :], nbg[:, nch - 1, 4:8], [nbg], [sfx])
            for c in range(nch - 2, -1, -1):
                self.tt("vector", sfx[:, c, :], nbg[:, c, 4:8], sfx[:, c + 1, :], ALU.add, [nbg, sfx], [sfx])
            self.tt("vector", self.wkf[:, 0:nch, :], wk[:, 0:nch, :], sfx[:, 0:nch, :], ALU.subtract, [wk, sfx], [self.wkf])
            self.act(l1[:, 0:nch, :], self.wkf[:, 0:nch, :], AF.Exp, [self.wkf], [l1])
            self.stt(self.wk2[:, 0:nch, :], l1[:, 0:nch, :], 128.0 ** -0.5,
                     self.vmask[:, cg0:cg0 + nch].unsqueeze(2).to_broadcast([128, nch, 4]), ALU.mult, ALU.mult,
                     [l1, self.vmask], [self.wk2])
            self.act(ebg[:, 0, 0:4], sfx[:, 0, :], AF.Exp, [sfx], [ebg], scale=-1.0)
            return
        self.act(l1[:, 0:nch, :], wk[:, 0:nch, :], AF.Exp, [wk], [l1])
        self.stt(self.wk2[:, 0:nch, :], l1[:, 0:nch, :], 128.0 ** -0.5,
                 self.vmask[:, cg0:cg0 + nch].unsqueeze(2).to_broadcast([128, nch, 4]), ALU.mult, ALU.mult,
                 [l1, self.vmask], [self.wk2])
        self.act(ebg[:, 0:nch, :], nbg[:, 0:nch, :], AF.Exp, [nbg], [ebg], scale=-1.0)

    def mlstm_prefix(self, nch):
        hb = [self.bank() for _ in range(4)]
        for c in range(nch):
            cs = slice(c * 128, (c + 1) * 128)
            wb4 = self.wk2[:, c, :].unsqueeze(2).to_broadcast([128, 4, 128])
            tbk = self.tbank()
            for hh in range(4):
                self.tr(tbk[:, hh * 128:(hh + 1) * 128], self.qkT[:, self.kbase + hh, cs], self.identb[:], [self.qkT, self.identb], [tbk])
            ktok = self.tb()
            kt4 = ktok[:, 0:512].rearrange("p (h d) -> p h d", h=4)
            self.tt("vector", kt4, tbk[:, 0:512].rearrange("p (h d) -> p h d", h=4), wb4, ALU.mult, [tbk, self.wk2], [ktok])
            for hh in range(4):
                self.mm(hb[hh][:, 0:129], kt4[:, hh, :], self.vext[:, c, hh, :], c == 0, c == nch - 1, [ktok, self.vext], [hb[hh]])
        ct = self.stB
        ct4 = ct[:, 0:516].rearrange("p (h e) -> p h e", h=4)
        self.tt("vector", ct4, self.Cst[:, :, :], self.ebg[:, 0, 0:4].unsqueeze(2).to_broadcast([128, 4, 129]), ALU.mult,
                [self.Cst, self.ebg], [ct])
        for hh in range(4):
            self.tt("vector", self.Cst[:, hh, :], hb[hh][:, 0:129], ct4[:, hh, :], ALU.add, [hb[hh], ct], [self.Cst])
        self.cp("scalar", self.Cbf[:, :, :], self.Cst[:, :, :], [self.Cst], [self.Cbf])
        yield

    def vext_cur_ap(self, c, hh):
        return self.vext[:, c, hh, :]

    def mlstm_chunk(self, c, full):
        cs = slice(c * 128, (c + 1) * 128)
        wb4 = self.wk2[:, c, :].unsqueeze(2).to_broadcast([128, 4, 128])
        tbk = self.tbank()
        for hh in range(4):
            self.tr(tbk[:, hh * 128:(hh + 1) * 128], self.qkT[:, self.kbase + hh, cs], self.identb[:], [self.qkT, self.identb], [tbk])
        if full:
            pbA = self.bank()
            for hh in range(4):
                self.mm(pbA[:, hh * 128:(hh + 1) * 128], self.qkT[:, 4 + hh, cs], self.qkT[:, hh, cs], True, True, [self.qkT], [pbA])
        ktok = self.tb()
        kt4 = ktok[:, 0:512].rearrange("p (h d) -> p h d", h=4)
        self.tt("vector", kt4, tbk[:, 0:512].rearrange("p (h d) -> p h d", h=4), wb4, ALU.mult, [tbk, self.wk2], [ktok])
        if full:
            at = self.tb()
            at4 = at[:, 0:512].rearrange("p (h d) -> p h d", h=4)
            self.tt("vector", at4, pbA[:, 0:512].rearrange("p (h d) -> p h d", h=4), wb4, ALU.mult, [pbA, self.wk2], [at])
            self.tt("vector", at4, at4, self.trif[:, :].unsqueeze(1).to_broadcast([128, 4, 128]), ALU.mult, [at, self.trif], [at])
        bP, bQ, bR = self.bank(), self.bank(), self.bank()
        numloc = [(bP, 0), (bP, 129), (bP, 258), (bQ, 0)]
        dcloc = [(bQ, 129), (bQ, 258), (bR, 0), (bR, 129)]
        if full:
            for hh in range(4):
                bk, o = numloc[hh]
                self.mm(bk[:, o:o + 129], at4[:, hh, :], self.vext[:, c, hh, :], True, False, [at, self.vext], [bk])
                self.mm(bk[:, o:o + 129], self.qkT[:, hh, cs], self.Cbf[:, hh, :], False, True, [self.qkT, self.Cbf], [bk])
        for hh in range(4):
            bk, o = dcloc[hh]
            self.mm(bk[:, o:o + 129], kt4[:, hh, :], self.vext[:, c, hh, :], True, True, [ktok, self.vext], [bk])
        if full:
            sm = self.sm
            for hh in range(4):
                bk, o = numloc[hh]
                self.act(sm[:, hh:hh + 1], bk[:, o + 128:o + 129], AF.Abs, [bk, self.ebg], [sm], scale=self.ebg[:, c, hh:hh + 1])
            self.ts("vector", sm[:, 4:8], sm[:, 0:4], 1.0, None, ALU.max, None, [sm], [sm])
            self.V(lambda E: E.reciprocal(out=sm[:, 8:12], in_=sm[:, 4:8]), [sm], [sm])
            self.tt("vector", sm[:, 12:16], sm[:, 8:12], self.ebg[:, c, 0:4], ALU.mult, [sm, self.ebg], [sm])
            for hh in range(4):
                bk, o = numloc[hh]
                self.stt(self.hg[:, hh * 128:(hh + 1) * 128], bk[:, o:o + 128], sm[:, 12 + hh:13 + hh],
                         self.sgo[:, c, hh * 128:(hh + 1) * 128], ALU.mult, ALU.mult, [bk, sm, self.sgo], [self.hg])
        ct = self.stB
        ct4 = ct[:, 0:516].rearrange("p (h e) -> p h e", h=4)
        self.tt("vector", ct4[:, 0:2, :], bQ[:, 129:387].rearrange("p (h e) -> p h e", h=2), self.Cst[:, 0:2, :], ALU.add,
                [bQ, self.Cst], [ct])
        self.tt("vector", ct4[:, 2:4, :], bR[:, 0:258].rearrange("p (h e) -> p h e", h=2), self.Cst[:, 2:4, :], ALU.add,
                [bR, self.Cst], [ct])
        self.tt("vector", self.Cst[:, :, :], ct4, self.ebg[:, c, 4:8].unsqueeze(2).to_broadcast([128, 4, 129]), ALU.mult,
                [ct, self.ebg], [self.Cst])
        self.cp("scalar", self.Cbf[:, :, :], self.Cst[:, :, :], [self.Cst], [self.Cbf])
        if full:
            hss = self.hss
            self.memset("vector", hss[:, 0:4], 0.0, [hss])
            for hh in range(4):
                junk = self.junk2
                self.act(junk[:, 0:128], self.hg[:, hh * 128:(hh + 1) * 128], AF.Square, [self.hg], [junk, hss],
                         accum_out=hss[:, hh:hh + 1])
            self.ts("vector", hss[:, 4:8], hss[:, 0:4], 1.0 / 128, 1e-6, ALU.mult, ALU.add, [hss], [hss])
            self.tt("gpsimd", hss[:, 8:12], hss[:, 4:8], self.consts[:, 2:3].to_broadcast([128, 4]), ALU.pow,
                    [hss, self.consts], [hss])
            ym = self.tb()
            for hh in range(4):
                self.stt(ym[:, hh * 128:(hh + 1) * 128], self.hg[:, hh * 128:(hh + 1) * 128], hss[:, 8 + hh:9 + hh],
                         self.mng[:, hh * 128:(hh + 1) * 128], ALU.mult, ALU.mult, [self.hg, hss, self.mng], [ym])
            tb2 = self.tbank()
            for hh in range(4):
                self.tr(tb2[:, hh * 128:(hh + 1) * 128], ym[:, hh * 128:(hh + 1) * 128], self.identb[:], [ym, self.identb], [tb2])
            self.cp("scalar", self.ymT[:, :, cs], tb2[:, 0:512].rearrange("p (k t) -> p k t", k=4), [tb2], [self.ymT])

    def s5_bounce(self, N):
        J = N // 8
        usT, Uall, X0, Xp = self.usT, self.Uall, self.X0, self.Xp
        if J == 64:
            self.dma(self.bq, self.scrA.t.rearrange("(a p) s j -> p a s j", p=128)[:, :, :, 0:J], usT[:, :, :, 0:J],
                     [usT], [self.scrA])
        else:
            for a_ in range(4):
                self.dma(self.bq, self.scrA.t[a_ * 128:(a_ + 1) * 128, :, 0:J], usT[:, a_, :, 0:J], [usT], [self.scrA])
        src = self.scrA.t.rearrange("(g i) s j -> s i g j", i=16)
        for s in range(8):
            self.dma(self.bq, Uall[s * 16:(s + 1) * 16, :, 0:J], src[s][:, :, 0:J], [self.scrA], [Uall])

    def s5_states(self, N, full):
        J = N // 8
        usT, Uall, X0, Xp = self.usT, self.Uall, self.X0, self.Xp
        for q in range(4):
            pb = self.bank()
            for pl in range(4):
                pr = q * 4 + pl
                for e in range(2):
                    g = pr * 2 + e
                    for ri, Ws in enumerate((self.WsRe, self.WsIm)):
                        c0 = (pl * 2 + ri) * 64
                        self.mm(pb[e * 64:(e + 1) * 64, c0:c0 + J], Ws[:, g, :], Uall[:, g, 0:J], True, True, [Ws, Uall], [pb])
            self.cp("scalar", X0[:, q * 4:(q + 1) * 4, :, 0:J],
                    pb[:, :].rearrange("p (a e j) -> p a e j", a=4, e=2)[:, :, :, 0:J], [pb], [X0])
        yield
        if full:
            yield from self.scan_full(J)
        else:
            self.dump("Xs", self.X0, self.X0[:, :, :, 0:J])
            self.scan_reduce(J)
            self.dump("Xr", self.X0, self.X0[:, :, :, 0:J])
            self.dump("carryR", self.carry, self.carry[:, :, :])

    def _cmul_small(self, outr, outi, ar, ai, xr, xi, rd, wr):
        t = self.tf()
        tv = t[:, 0:64].rearrange("p (k a) -> p k a", k=4)
        self.tt("vector", tv[:, 0, :], ar, xr, ALU.mult, rd, [t])
        self.tt("vector", tv[:, 1, :], ai, xi, ALU.mult, rd, [t])
        self.tt("vector", tv[:, 2, :], ar, xi, ALU.mult, rd, [t])
        self.tt("vector", tv[:, 3, :], ai, xr, ALU.mult, rd, [t])
        self.tt("vector", outr, tv[:, 0, :], tv[:, 1, :], ALU.subtract, [t], wr)
        self.tt("vector", outi, tv[:, 2, :], tv[:, 3, :], ALU.add, [t], wr)

    def scan_full(self, J):
        X0, Xp = self.X0, self.Xp
        t = self.tf()
        tv = t[:, 64:128].rearrange("p (k a) -> p k a", k=4)
        self._cmul_small(tv[:, 0, :], tv[:, 1, :], self.cpr[:, 0, :], self.cpi[:, 0, :], self.carry[:, :, 0], self.carry[:, :, 1],
                         [self.cpr, self.cpi, self.carry], [t])
        self.tt("vector", X0[:, :, 0, 0], X0[:, :, 0, 0], tv[:, 0, :], ALU.add, [X0, t], [X0])
        self.tt("vector", X0[:, :, 1, 0], X0[:, :, 1, 0], tv[:, 1, :], ALU.add, [X0, t], [X0])
        self.cp("scalar", Xp[:, :, :, 0], self.carry[:, :, :], [self.carry], [Xp])
        k = 0
        dsh = 1
        A, B = self.stA, self.stB
        while dsh < J:
            n = J - dsh
            for hf in range(2):
                ps_ = slice(hf * 8, (hf + 1) * 8)
                a1 = A[:, 0:16 * n].rearrange("p (a e j) -> p a e j", a=8, e=2)
                a2 = B[:, 0:16 * n].rearrange("p (a e j) -> p a e j", a=8, e=2)
                xr, xi = X0[:, ps_, 0, 0:n], X0[:, ps_, 1, 0:n]
                ar3 = self.cpr[:, k, ps_].unsqueeze(2).to_broadcast([128, 8, n])
                ai3 = self.cpi[:, k, ps_].unsqueeze(2).to_broadcast([128, 8, n])
                self.tt("vector", a2[:, :, 1, :], xi, ai3, ALU.mult, [X0.s(hf, 1), self.cpi], [B.s("im")])
                self.tt("vector", a1[:, :, 0, :], xr, ar3, ALU.mult, [X0.s(hf, 0), self.cpr], [A.s("re")])
                self.tt("vector", a2[:, :, 0, :], xr, ai3, ALU.mult, [X0.s(hf, 0), self.cpi], [B.s("re")])
                self.tt("vector", a1[:, :, 1, :], xi, ar3, ALU.mult, [X0.s(hf, 1), self.cpr], [A.s("im")])
                self.tt("vector", a1[:, :, 0, :], a1[:, :, 0, :], a2[:, :, 1, :], ALU.subtract, [A.s("re"), B.s("im")], [A.s("re")])
                self.tt("vector", a1[:, :, 1, :], a1[:, :, 1, :], a2[:, :, 0, :], ALU.add, [A.s("im"), B.s("re")], [A.s("im")])
                self.tt("vector", X0[:, ps_, 0, dsh:J], X0[:, ps_, 0, dsh:J], a1[:, :, 0, :], ALU.add, [X0.s(hf, 0), A.s("re")], [X0.s(hf, 0)])
                self.tt("vector", X0[:, ps_, 1, dsh:J], X0[:, ps_, 1, dsh:J], a1[:, :, 1, :], ALU.add, [X0.s(hf, 1), A.s("im")], [X0.s(hf, 1)])
                yield
            dsh *= 2
            k += 1
        if J > 1:
            self.cp("scalar", Xp[:, :, :, 1:J], X0[:, :, :, 0:J - 1], [X0], [Xp])
        self.cp("vector", self.carry[:, :, :], X0[:, :, :, J - 1], [X0], [self.carry])

    def scan_reduce(self, J):
        X0 = self.X0
        A, B = self.stA, self.stB
        xe = lambda e: X0.s(0, e) + X0.s(1, e)
        k = 0
        n = J // 2
        while n >= 1:
            a1 = A[:, 0:32 * n].rearrange("p (a e j) -> p a e j", a=16, e=2)
            a2 = B[:, 0:32 * n].rearrange("p (a e j) -> p a e j", a=16, e=2)
            evr, evi = X0[:, :, 0, 0:2 * n:2], X0[:, :, 1, 0:2 * n:2]
            odr, odi = X0[:, :, 0, 1:2 * n:2], X0[:, :, 1, 1:2 * n:2]
            ar3 = self.cpr[:, k, :].unsqueeze(2).to_broadcast([128, 16, n])
            ai3 = self.cpi[:, k, :].unsqueeze(2).to_broadcast([128, 16, n])
            self.tt("vector", a1[:, :, 0, :], evr, ar3, ALU.mult, [xe(0), self.cpr], [A.s("re")])
            self.tt("vector", a2[:, :, 1, :], evi, ai3, ALU.mult, [xe(1), self.cpi], [B.s("im")])
            self.tt("vector", a2[:, :, 0, :], evr, ai3, ALU.mult, [xe(0), self.cpi], [B.s("re")])
            self.tt("vector", a1[:, :, 1, :], evi, ar3, ALU.mult, [xe(1), self.cpr], [A.s("im")])
            self.tt("vector", a1[:, :, 0, :], a1[:, :, 0, :], a2[:, :, 1, :], ALU.subtract, [A.s("re"), B.s("im")], [A.s("re")])
            self.tt("vector", a1[:, :, 1, :], a1[:, :, 1, :], a2[:, :, 0, :], ALU.add, [A.s("im"), B.s("re")], [A.s("im")])
            self.tt("vector", a1[:, :, 0, :], a1[:, :, 0, :], odr, ALU.add, [A.s("re"), xe(0)], [A.s("re")])
            self.tt("vector", a1[:, :, 1, :], a1[:, :, 1, :], odi, ALU.add, [A.s("im"), xe(1)], [A.s("im")])
            self.cp("vector", X0[:, :, 0, 0:n], a1[:, :, 0, :], [A.s("re")], [xe(0)])
            self.cp("vector", X0[:, :, 1, 0:n], a1[:, :, 1, :], [A.s("im")], [xe(1)])
            n //= 2
            k += 1
        t = self.tf()
        tv = t[:, 64:128].rearrange("p (k a) -> p k a", k=4)
        self._cmul_small(tv[:, 0, :], tv[:, 1, :], self.cpr[:, k, :], self.cpi[:, k, :], self.carry[:, :, 0], self.carry[:, :, 1],
                         [self.cpr, self.cpi, self.carry], [t])
        self.tt("vector", self.carry[:, :, 0], tv[:, 0, :], X0[:, :, 0, 0], ALU.add, [t, X0], [self.carry])
        self.tt("vector", self.carry[:, :, 1], tv[:, 1, :], X0[:, :, 1, 0], ALU.add, [t, X0], [self.carry])

    def s5_out(self, N, glu_w):
        J = N // 8
        Uall, Xp, Ygl = self.Uall, self.Xp, self.Ygl
        for q in range(4):
            pb = self.bank()
            for gl in range(8):
                g = q * 8 + gl
                pr, e = g // 2, g % 2
                o = pb[:, gl * 64:gl * 64 + J]
                self.mm(o, self.Mw[:, g, :], Uall[:, g, 0:J], True, False, [self.Mw, Uall], [pb])
                self.mm(o, self.VwRe[e * 64:(e + 1) * 64, pr, :], Xp[e * 64:(e + 1) * 64, pr, 0, 0:J], False, False,
                        [self.VwRe, Xp], [pb])
                self.mm(o, self.VwIm[e * 64:(e + 1) * 64, pr, :], Xp[e * 64:(e + 1) * 64, pr, 1, 0:J], False, True,
                        [self.VwIm, Xp], [pb])
            pv = pb[:, :].rearrange("p (g j) -> p g j", g=8)[:, :, 0:J]
            self.gelu_from(pv, Ygl[:, q * 8:(q + 1) * 8, 0:J], [pb], [Ygl], [128, 8, J])
            yield
        dst = self.scrB.t.rearrange("(g o) r j -> r o g j", o=16)
        for r in range(8):
            self.dma(self.bq, dst[r][:, :, 0:J], Ygl[r * 16:(r + 1) * 16, :, 0:J], [Ygl], [self.scrB])
        ygT = self.usT
        if J == 64:
            self.dma(self.bq, ygT[:, :, :, 0:J], self.scrB.t.rearrange("(a p) r j -> p a r j", p=128)[:, :, :, 0:J],
                     [self.scrB], [ygT])
        else:
            for a_ in range(4):
                self.dma(self.bq, ygT[:, a_, :, 0:J], self.scrB.t[a_ * 128:(a_ + 1) * 128, :, 0:J], [self.scrB], [ygT])
        def yv(k):
            if J == 64:
                return ygT[:, k, :, :].rearrange("p r j -> p (r j)")
            return None
        for mt in range(4):
            pb = self.bank()
            if J == 64:
                for k in range(4):
                    self.mm(pb[:, 0:N], glu_w[:, k, mt * 128:(mt + 1) * 128], yv(k), k == 0, k == 3, [self.wglu_slot, ygT], [pb])
            else:
                for r in range(8):
                    for k in range(4):
                        self.mm(pb[:, r * J:(r + 1) * J], glu_w[:, k, mt * 128:(mt + 1) * 128], ygT[:, k, r, 0:J], k == 0, k == 3,
                                [self.wglu_slot, ygT], [pb])
            th = self.tf()
            self.act(th[:, 0:N], pb[:, 0:N], AF.Tanh, [pb, self.bgluT], [th], scale=0.5, bias=self.bgluT[:, mt:mt + 1])
            ov = self.ys5T[:, mt, 0:N].rearrange("p (r j) -> p r j", r=8)
            self.stt(ov, th[:, 0:N].rearrange("p (r j) -> p r j", r=8), 1.0, ygT[:, mt, :, 0:J], ALU.add, ALU.mult,
                     [th, ygT], [self.ys5T])
            yield

    def gelu_from(self, pin, out, r, w, shp):
        n = int(np.prod(shp[1:]))

        def vw(t):
            ap = t[:, 0:n]
            if len(shp) == 3:
                ap = ap.rearrange("p (a b) -> p a b", a=shp[1])
            return ap
        xh, sq, th = self.tf(), self.tf(), self.tf()
        self.act(vw(xh), pin, AF.Copy, r, [xh], scale=0.5)
        self.gelu_half(vw(xh), xh, vw(sq), sq, vw(th), th, out, w)

    def gelu_half(self, xh, xh_t, sq, sq_t, th, th_t, out, w, mul=None, mul_t=None):
        self.act(sq, xh, AF.Square, [xh_t], [sq_t], scale=math.sqrt(2 * GC * 4 * 0.044715))
        self.stt(th, sq, 2 * GC, xh, ALU.add, ALU.mult, [sq_t, xh_t], [th_t])
        self.act(sq, th, AF.Tanh, [th_t], [sq_t])
        if mul is None:
            self.stt(out, sq, 1.0, xh, ALU.add, ALU.mult, [sq_t, xh_t], w)
        else:
            self.stt(th, sq, 1.0, xh, ALU.add, ALU.mult, [sq_t, xh_t], [th_t])
            self.tt("vector", out, th, mul, ALU.mult, [th_t, mul_t], w)

    def resid_evac(self, pb, c, half, scale):
        dst = self.xt[:, c, half * 512:(half + 1) * 512]
        self.stt(dst, pb[:, 0:512], scale, dst, ALU.mult, ALU.add, [pb, self.xt.s(c, half)], [self.xt.s(c, half)])

    def proj_tokmajor_resid(self, srcT, kt, wname, nch, scale):
        for half in range(2):
            slot, wv = self.wslab(self.wdep(wname, half * 512, (half + 1) * 512), self.w3(wname, half * 512, (half + 1) * 512)[:, 0:kt, :], kt, 512)
            for c in range(nch):
                pb = self.bank()
                self.tm_matmul(pb, srcT, c, wv, 0, 512, kt, [srcT, slot])
                self.resid_evac(pb, c, half, scale)
                yield

    def load_x(self, xt, tok0, N):
        self.dma(self.bq, xt[:, 0:N // 128, :], self.xs[tok0:tok0 + N, :].rearrange("(c p) d -> p c d", p=128), [self.xs], [xt])

    def tile(self, tok0, N, full, out_row, first=True, nxt=None, ahead=None):
        self.wphase = "front"
        nch = N // 128
        J = N // 8
        cg0 = tok0 // 128
        hT = self.hT
        if ahead is not None:
            specs_, i_ = ahead
            p_own = 0 if self.xt is self.xtb[0] else 1
            p_oth = 1 - p_own
            if i_ == 0:
                self.load_x(self.xtb[p_own], tok0, N)
                if len(specs_) > 1:
                    self.load_x(self.xtb[p_oth], specs_[1][0], specs_[1][1])
                self.norm_T(nch, self.g1T)
                if len(specs_) > 2:
                    self.load_x(self.xtb[p_own], specs_[2][0], specs_[2][1])
            if i_ + 1 < len(specs_):
                self.set_par(p_oth)
                self.norm_T(specs_[i_ + 1][1] // 128, self.g1T)
                self.set_par(p_own)
                if i_ + 3 < len(specs_):
                    self.load_x(self.xtb[p_oth], specs_[i_ + 3][0], specs_[i_ + 3][1])
            self.precast_flush(self.pc_per, [self.rstd])
        else:
            if first:
                self.load_x(self.xt, tok0, N)
            if nxt is not None:
                other = self.xtb[1] if self.xt is self.xtb[0] else self.xtb[0]
                self.load_x(other, nxt[0], nxt[1])
        if full and N == 512:
            self.dump("carry0", self.carry, self.carry[:, :, :])
            self.dump("Cst0", self.Cst, self.Cst[:, :, :])
        if ahead is None:
            self.norm_T(nch, self.g1T)
        if full:
            self.dump("hT", self.hT, self.hT[:, :, 0:N])
        wi = None
        if full:
            uslot, uw = self.wslab(self.wdep("w_in", 0, 512), self.w3("w_in", 0, 512), 8, 512)
        else:
            uslot, uw = self.res_u
        for mt in range(4):
            pb = self.bank()
            self.fm_matmul(pb, uw, mt * 128, hT, 8, N, [uslot, hT])
            self.cp("scalar", self.usT[:, mt, :, 0:J], pb[:, 0:N].rearrange("p (j s) -> p s j", s=8), [pb], [self.usT])
            yield
        if full:
            self.dump("usT", self.usT, self.usT[:, :, :, 0:J])
        self.s5_bounce(N)
        if full:
            qslot, qw = self.wslab(self.wdep("w_in", 512, 1024), self.w3("w_in", 512, 1024), 8, 512)
            for mp in (0, 2):
                items = []
                for mt in (mp, mp + 1):
                    pb = self.bank()
                    self.fm_matmul(pb, qw, mt * 128, hT, 8, N, [qslot, hT])
                    items.append((pb, mt, mt))
                self.conv_silu2(items, N)
                yield
            kslot, kw = self.wslab(self.wdep("w_in", 1024, 1536), self.w3("w_in", 1024, 1536), 8, 512)
        else:
            kslot, kw = self.res_k
        for mp in (0, 2):
            items = []
            for mt in (mp, mp + 1):
                pb = self.bank()
                self.fm_matmul(pb, kw, mt * 128, hT, 8, N, [kslot, hT])
                items.append((pb, 4 + mt, self.kbase + mt))
            self.conv_silu2(items, N)
            yield
        if full:
            vslot, vw = self.wslab(self.wdep("w_in", 1536, 2048), self.w3("w_in", 1536, 2048), 8, 512)
        else:
            vslot, vw = self.res_v
        for c in range(nch):
            pb = self.bank()
            self.tm_matmul(pb, hT, c, vw, 0, 512, 8, [hT, vslot])
            self.cp("scalar", self.vext[:, c, :, 0:128], pb[:, 0:512].rearrange("p (h d) -> p h d", h=4), [pb], [self.vext])
            yield
        if full:
            oslot, ow = self.wslab(self.wdep("w_in", 2048, 2560), self.w3("w_in", 2048, 2560), 8, 512)
            for c in range(nch):
                pb = self.bank()
                self.tm_matmul(pb, hT, c, ow, 0, 512, 8, [hT, oslot])
                th = self.tf()
                self.act(th[:, 0:512], pb[:, 0:512], AF.Tanh, [pb], [th], scale=0.5)
                self.ts("vector", self.sgo[:, c, :], th[:, 0:512], 0.5, 0.5, ALU.mult, ALU.add, [th], [self.sgo])
                yield
        self.gates_all(nch, cg0, fold=not full)
        if not full:
            yield "MARK"
            yield from self.s5_states(N, full)
            yield from self.mlstm_prefix(nch)
            return
        self.memset("vector", self.vext[:, :, :, 128:129], 1.0, [self.vext])
        for c in range(nch):
            self.mlstm_chunk(c, full)
            yield
        self.dump("ymT", self.ymT, self.ymT[:, :, 0:N])
        for hf in range(2):
            mslot, mw = self.wslab(self.wdep("w_br_m", hf * 512, (hf + 1) * 512), self.w3("w_br_m", hf * 512, (hf + 1) * 512), 4, 512)
            g2slot, g2w = self.wslab(self.wdep("w_in", 2568 + 1024 + hf * 512, 2568 + 1024 + (hf + 1) * 512),
                                     self.w3("w_in", 2568 + 1024 + hf * 512, 2568 + 1024 + (hf + 1) * 512), 8, 512)
            for ml in range(4):
                mt = hf * 4 + ml
                pg = self.bank()
                self.fm_matmul(pg, g2w, ml * 128, hT, 8, N, [g2slot, hT])
                th = self.tf()
                self.act(th[:, 0:N], pg[:, 0:N], AF.Tanh, [pg, self.bgT], [th], scale=0.5, bias=self.bgT[:, 8 + mt:9 + mt])
                pbr = self.bank()
                self.fm_matmul(pbr, mw, ml * 128, self.ymT, 4, N, [mslot, self.ymT])
                self.stt(self.mergedT[:, mt, 0:N], th[:, 0:N], 1.0, pbr[:, 0:N], ALU.add, ALU.mult, [th, pbr], [self.mergedT])
                yield
        yield from self.s5_states(N, full)
        gslot, gw = self.wslab(self.wdep("s5_w_glu", 0, 512), self.w3("s5_w_glu", 0, 512), 4, 512)
        self.wglu_slot = gslot
        yield from self.s5_out(N, gw)
        self.dump("ys5T", self.ys5T, self.ys5T[:, :, 0:N])
        for hf in range(2):
            s5slot, s5w = self.wslab(self.wdep("w_br_s5", hf * 512, (hf + 1) * 512), self.w3("w_br_s5", hf * 512, (hf + 1) * 512), 4, 512)
            g1slot, g1w = self.wslab(self.wdep("w_in", 2568 + hf * 512, 2568 + (hf + 1) * 512),
                                     self.w3("w_in", 2568 + hf * 512, 2568 + (hf + 1) * 512), 8, 512)
            for ml in range(4):
                mt = hf * 4 + ml
                pg = self.bank()
                self.fm_matmul(pg, g1w, ml * 128, hT, 8, N, [g1slot, hT])
                th = self.tf()
                self.act(th[:, 0:N], pg[:, 0:N], AF.Tanh, [pg, self.bgT], [th], scale=0.5, bias=self.bgT[:, mt:mt + 1])
                pbr = self.bank()
                self.fm_matmul(pbr, s5w, ml * 128, self.ys5T, 4, N, [s5slot, self.ys5T])
                m1 = self.tf()
                self.stt(m1[:, 0:N].rearrange("p (j r) -> p j r", r=8), th[:, 0:N].rearrange("p (j r) -> p j r", r=8), 1.0,
                         pbr[:, 0:N].rearrange("p (r j) -> p j r", r=8), ALU.add, ALU.mult, [th, pbr], [m1])
                self.stt(self.mergedT[:, mt, 0:N], m1[:, 0:N], 0.5, self.mergedT[:, mt, 0:N], ALU.mult, ALU.add,
                         [m1, self.mergedT], [self.mergedT])
                yield
        self.dump("mergedT", self.mergedT, self.mergedT[:, :, 0:N])
        yield from self.proj_tokmajor_resid(self.mergedT, 8, "w_out", nch, 0.5)
        if self.dbg == "x1":
            return self.store_xt(nch, N, out_row)
        self.norm_T(nch, self.g2T)
        qxT = self.qkT
        for hf in range(2):
            slot, wv = self.wslab(self.wdep("x_wq", hf * 512, (hf + 1) * 512), self.w3("x_wq", hf * 512, (hf + 1) * 512), 8, 512)
            for ml in range(4):
                pb = self.bank()
                self.fm_matmul(pb, wv, ml * 128, hT, 8, N, [slot, hT])
                self.cp("scalar", qxT[:, hf * 4 + ml, 0:N], pb[:, 0:N], [pb], [qxT])
                yield
        oxT = self.mergedT
        for hh in range(4):
            PT = self.tb()
            for mtile in range(2):
                pb = self.bank()
                for dd in range(2):
                    self.mm(pb[:, 0:N], self.kxT[:, hh * 2 + dd, mtile * 128:(mtile + 1) * 128], qxT[:, hh * 2 + dd, 0:N],
                            dd == 0, dd == 1, [self.kxT, qxT], [pb])
                self.act(PT[:, mtile * 512:mtile * 512 + N], pb[:, 0:N], AF.Exp, [pb], [PT], scale=1.0 / 16)
            pbs = self.bank()
            for mtile in range(2):
                self.mm(pbs[:, 0:N], self.onesb[:], PT[:, mtile * 512:mtile * 512 + N], mtile == 0, mtile == 1, [self.onesb, PT], [pbs])
            rec = self.tf()
            self.V(lambda E, rec=rec, pbs=pbs: E.reciprocal(out=rec[:, 0:N], in_=pbs[:, 0:N]), [pbs], [rec])
            for dvt in range(2):
                pbo = self.bank()
                for mtile in range(2):
                    c0 = hh * 256 + dvt * 128
                    self.mm(pbo[:, 0:N], self.vx[:, mtile, c0:c0 + 128], PT[:, mtile * 512:mtile * 512 + N],
                            mtile == 0, mtile == 1, [self.vx, PT], [pbo])
                self.tt("vector", oxT[:, hh * 2 + dvt, 0:N], pbo[:, 0:N], rec[:, 0:N], ALU.mult, [pbo, rec], [oxT])
            yield
        yield from self.proj_tokmajor_resid(oxT, 8, "x_wo", nch, 1.0)
        if self.dbg == "x2":
            return self.store_xt(nch, N, out_row)
        self.norm_T(nch, self.g3T)
        yield "MARK"
        self.wphase = "back"
        for (i0, i1) in ((0, 3), (3, 6), (6, 9), (9, 11)):
            mt0 = i0 * 2
            nk = (i1 - i0) * 2
            for i in range(i0, i1):
                self.wphase = "back"
                aslot, aw = self.wslab(self.wdep("f_w_up", i * 256, (i + 1) * 256), self.w3("f_w_up", i * 256, (i + 1) * 256), 8, 256, half=0)
                bslot, bw = self.wslab(self.wdep("f_w_up", 2816 + i * 256, 2816 + (i + 1) * 256),
                                       self.w3("f_w_up", 2816 + i * 256, 2816 + (i + 1) * 256), 8, 256, half=1)
                st1 = []
                for e in range(2):
                    mt = i * 2 + e
                    up_s = self.uprev.s(mt % 2)
                    pbs_ = []
                    for (slot, wv, ch) in ((aslot, aw, mt), (bslot, bw, 22 + mt)):
                        pb = self.bank()
                        self.fm_matmul(pb, wv, e * 128, hT, 8, N, [slot, hT])
                        pbs_.append((pb, ch))
                    if out_row is None:
                        for pb, ch in pbs_:
                            self.cp("scalar", self.uprev[:, ch, :], pb[:, N - 2:N], [pb], [up_s])
                        continue
                    raws = [self.tf(), self.tf()]
                    accs = [self.tf(), self.tf()]
                    for j_, (pb, ch) in enumerate(pbs_):
                        self.cp("scalar", raws[j_][:, 0:2], self.uprev[:, ch, :], [up_s], [raws[j_]])
                        self.cp("scalar", raws[j_][:, 2:2 + N], pb[:, 0:N], [pb], [raws[j_]])
                        self.act(accs[j_][:, 0:N], pb[:, 0:N], AF.Identity, [pb, self.fcw, self.fcb], [accs[j_]],
                                 scale=self.fcw[:, ch, 2:3], bias=self.fcb[:, ch:ch + 1])
                        self.cp("scalar", self.uprev[:, ch, :], raws[j_][:, N:N + 2], [raws[j_]], [up_s])
                    for k in (1, 0):
                        for j_, (pb, ch) in enumerate(pbs_):
                            self.stt(accs[j_][:, 0:N], raws[j_][:, k:k + N], self.fcw[:, ch, k:k + 1], accs[j_][:, 0:N], ALU.mult, ALU.add,
                                     [raws[j_], self.fcw, accs[j_]], [accs[j_]])
                    st1.append((mt, raws, accs))
                for (mt, raws, accs) in st1:
                    ah, bcv = accs
                    ra, rb = raws
                    self.act(rb[:, 0:N], ah[:, 0:N], AF.Square, [ah], [rb], scale=math.sqrt(2 * GC * 4 * 0.044715))
                    self.tt("vector", ra[:, 0:N], ah[:, 0:N], bcv[:, 0:N], ALU.mult, [ah, bcv], [ra])
                    self.stt(bcv[:, 0:N], rb[:, 0:N], 2 * GC, ah[:, 0:N], ALU.add, ALU.mult, [rb, ah], [bcv])
                    self.act(rb[:, 0:N], bcv[:, 0:N], AF.Tanh, [bcv], [rb])
                    self.stt(self.gT[:, mt - mt0, 0:N], rb[:, 0:N], 1.0, ra[:, 0:N], ALU.add, ALU.mult, [rb, ra], [self.gT])
                yield
            if out_row is None:
                continue
            for half in range(2):
                self.wphase = "back"
                pbs = [self.bank() for _ in range(nch)]
                srcw = self.wb["f_w_down"].t[mt0 * 128:(mt0 + nk) * 128, half * 512:(half + 1) * 512].rearrange("(k p) n -> p k n", p=128)
                slot, wv = self.wslab(self.wdep("f_w_down", half * 512, (half + 1) * 512), srcw, nk, 512)
                for c in range(nch):
                    for k in range(nk):
                        self.mm(pbs[c][:, 0:512], self.gT[:, k, c * 128:(c + 1) * 128], wv[:, k, :],
                                k == 0, k == nk - 1, [self.gT, slot], [pbs[c]])
                for c in range(nch):
                    self.resid_evac(pbs[c], c, half, 1.0)
                yield
        if out_row is None:
            self.act(self.uprev[:], self.uprev[:], AF.Copy, [self.uprev, self.hvalid], [self.uprev], scale=self.hvalid[:, 0:1])
            return
        if self.dbg == "x3":
            return self.store_xt(nch, N, out_row)
        xt, ss, rstd = self.xt, self.ss, self.rstd
        self.memset("vector", ss[:], 0.0, [ss])
        for c in range(nch):
            junk = self.tb()
            self.act(junk[:], xt[:, c, :], AF.Square, [xt.s(c)], [junk, ss], accum_out=ss[:, c:c + 1])
        self.ts("vector", rstd[:, 0:nch], ss[:, 0:nch], 1.0 / D, 1e-6, ALU.mult, ALU.add, [ss], [rstd])
        self.tt("gpsimd", rstd[:, 0:nch], rstd[:, 0:nch], self.consts[:, 2:3].to_broadcast([128, nch]), ALU.pow,
                [rstd, self.consts], [rstd])
        for half in range(2):
            gf = self.tf()
            self.dma("gpsimd", gf[:, 0:512], self.din["final_norm_g"].t[half * 512:(half + 1) * 512].partition_broadcast(128), [], [gf])
            for c in range(nch):
                dst = xt[:, c, half * 512:(half + 1) * 512]
                self.stt(dst, dst, rstd[:, c:c + 1], gf[:, 0:512], ALU.mult, ALU.mult, [xt.s(c, half), rstd, gf], [xt.s(c, half)])
        self.dma("gpsimd", self.out[out_row:out_row + N, :].rearrange("(c p) d -> p c d", p=128), xt[:, 0:nch, :], [xt], [self.out])
        yield

    def store_xt(self, nch, N, out_row):
        if out_row is not None:
            self.dma("sync", self.out[out_row:out_row + N, :].rearrange("(c p) d -> p c d", p=128), self.xt[:, 0:nch, :],
                     [self.xt], [self.out])

    def dump(self, name, src_t, src_ap):
        if not self.dbg or name in self.dumps:
            return
        o = self.dram("dbg_" + name, [int(x) for x in src_ap.shape], F32, "ExternalOutput")
        self.dma("gpsimd", o.t, src_ap, [src_t], [o])
        self.dumps[name] = o

    def mem_kv(self):
        self.dma("sync", self.xt[:, 0:2, :], self.mem.t.rearrange("(c p) d -> p c d", p=128), [self.mem], [self.xt])
        self.norm_T(2, self.gmT)
        for hf in range(2):
            slot, wv = self.wslab(self.wdep("x_wkv", hf * 512, (hf + 1) * 512), self.w3("x_wkv", hf * 512, (hf + 1) * 512), 8, 512)
            for ml in range(4):
                pb = self.bank()
                self.fm_matmul(pb, wv, ml * 128, self.hT, 8, 256, [slot, self.hT])
                self.cp("vector", self.kxT[:, hf * 4 + ml, :], pb[:, 0:256], [pb], [self.kxT])
        for hf in range(2):
            slot, wv = self.wslab(self.wdep("x_wkv", 1024 + hf * 512, 1024 + (hf + 1) * 512), self.w3("x_wkv", 1024 + hf * 512, 1024 + (hf + 1) * 512), 8, 512)
            for c in range(2):
                pb = self.bank()
                self.tm_matmul(pb, self.hT, c, wv, 0, 512, 8, [self.hT, slot])
                self.cp("vector", self.vx[:, c, hf * 512:(hf + 1) * 512], pb[:, 0:512], [pb], [self.vx])

    def run_pipeline(self, specs, pipelined=True, K=3, alt=False):
        gens = [{"gen": self.tile(*sp, first=True, nxt=None, ahead=((specs, i) if alt else None)),
                 "par": (i + 1) % 2, "par2": (i % 2) if alt else 0, "phase": "front",
                 "bq": ("gpsimd" if (sp[2] and sp[3] is not None) else "sync")}
                for i, sp in enumerate(specs)]

        def step(g):
            self.set_par(g["par"])
            self.set_par2(g["par2"])
            self.wphase = g["phase"]
            self.bq = g["bq"]
            try:
                r = next(g["gen"])
            except StopIteration:
                return "END"
            if r == "MARK":
                g["phase"] = "back"
            return r

        if not pipelined:
            for g in gens:
                while step(g) != "END":
                    pass
            return
        active = None
        for g in gens:
            if active is None:
                active = g
                while True:
                    r = step(active)
                    if r in ("MARK", "END"):
                        break
                if r == "END":
                    active = None
                continue
            g_state = None
            while True:
                r = step(active)
                if r == "END":
                    break
                if g_state is None:
                    for _ in range(K):
                        r2 = step(g)
                        if r2 in ("MARK", "END"):
                            g_state = r2
                            break
            if g_state is None:
                while True:
                    r2 = step(g)
                    if r2 in ("MARK", "END"):
                        g_state = r2
                        break
            active = None if g_state == "END" else g
        if active is not None:
            while step(active) != "END":
                pass

    def build(self, n_pre_tiles=12, n_main_tiles=4, do_halo=True, pipelined=True):
        self.dumps = {}
        self.declare_io()
        self.wblocks = {}
        self.pc_jobs = []
        self.alloc_core()
        self.precast_cols("w_in", [(0, 512), (1024, 1536), (1536, 2048), (2560, 2568)])
        d = self.din
        self.dma("sync", self.identf[:], d["c_ident"][:, :], [], [self.identf])
        self.memset("vector", self.consts[:, 0:1], 1.0, [self.consts])
        self.memset("vector", self.consts[:, 1:2], math.pi / 2, [self.consts])
        self.memset("vector", self.consts[:, 2:3], -0.5, [self.consts])
        self.memset("vector", self.consts[:, 3:4], 0.0, [self.consts])
        self.s5_setup()
        self.alloc_work()
        self.load_consts()
        self.precast_cols("w_in", [(512, 1024), (2048, 2560)], True)
        self.precast_cols("s5_w_glu", [(0, 512)], True)
        self.precast_cols("x_wkv", [(i * 512, (i + 1) * 512) for i in range(4)], True)
        self.precast_cols("w_br_s5", [(0, 512), (512, 1024)], True)
        self.precast_cols("w_in", [(2568 + i * 512, 2568 + (i + 1) * 512) for i in range(4)], True)
        self.precast_cols("w_br_m", [(0, 512), (512, 1024)], True)
        for nm_ in ("w_out", "x_wq", "x_wo"):
            self.precast_cols(nm_, [(0, 512), (512, 1024)], True)
        self.precast_cols("f_w_up", [(i * 512, (i + 1) * 512) for i in range(11)], True)
        self.precast_cols("f_w_down", [(0, 512), (512, 1024)], True)
        self.dma("sync", self.ifslab[:], self.w3("w_in", 2560, 2568), self.wdep("w_in", 2560, 2568), [self.ifslab])
        if n_pre_tiles:
            self.res_u = self.wslab(self.wdep("w_in", 0, 512), self.w3("w_in", 0, 512), 8, 512, slot=0)
            self.res_k = self.wslab(self.wdep("w_in", 1024, 1536), self.w3("w_in", 1024, 1536), 8, 512, slot=1)
            self.res_v = self.wslab(self.wdep("w_in", 1536, 2048), self.w3("w_in", 1536, 2048), 8, 512, slot=2)
        n_early = len([j for j in self.pc_jobs if not j[0].startswith("f_w_")])
        self.pc_per = (len(self.pc_jobs) + max(n_pre_tiles, 1) - 1) // max(n_pre_tiles, 1)
        pre = [(i * 512, 512, False, None) for i in range(12 - n_pre_tiles, 12)]
        self.run_pipeline(pre, pipelined, K=4, alt=True)
        self.set_par2(0)
        self.precast_flush(1000)
        self.set_par(0)
        self.wphase = "front"
        self.mem_kv()
        specs = []
        if do_halo:
            specs.append((NPRE, 128, True, None))
        for i in range(n_main_tiles):
            specs.append((NPRE + NHALO + i * 512, 512, True, i * 512))
        self.run_pipeline(specs, pipelined)
        self.mk.wait_bufs("sync", [self.out] + list(self.dumps.values()))
        self.mk.emit()
        self.mk.close()


def build_nc(dbg=None, **kw):
    nc = bass.Bass("TRN2", target_bir_lowering=False)
    es = contextlib.ExitStack()
    kb = KB(nc, es, dbg)
    with es:
        kb.build(**kw)
    return nc, kb


def make_consts():
    idx = np.arange(128)
    c = {
        "c_ident": np.eye(128, dtype=np.float32),
        "c_tri": (idx[:, None] <= idx[None, :]).astype(np.float32),
        "c_ones": np.ones((128, 128), np.float32),
        "c_mmask": ((idx[None, :] // 16) >= (idx[:, None] // 16)).astype(np.float32),
    }
    return c


def core_inputs(inputs, core):
    b, s = core // 4, core % 4
    x = inputs["x"]
    xs = np.zeros((NTOK, D), np.float32)
    valid = np.zeros((NTOK,), np.float32)
    end = (s + 1) * SEG
    start = end - NTOK
    lo = max(start, 0)
    xs[lo - start:] = x[b, lo:end]
    valid[lo - start:] = 1.0
    m = {"xs": xs, "mem": np.ascontiguousarray(inputs["mem"][b])}
    m["vmask"] = np.ascontiguousarray(valid.reshape(NCHT, 128).T)
    m["hvalid"] = np.full((128, 1), 1.0 if s > 0 else 0.0, np.float32)
    for name, r, c in W_SPECS:
        m[name] = np.ascontiguousarray(inputs[name][0])
    for name, shape in SMALL_SPECS:
        if name.startswith("c_") or name in ("vmask", "hvalid"):
            continue
        a = inputs[name]
        if name != "final_norm_g":
            a = a[0]
        m[name] = np.ascontiguousarray(a.reshape(shape))
    m.update(make_consts())
    return m


_NC_CACHE = {}


def kernel(**inputs):
    inputs = {k: np.asarray(v) for k, v in inputs.items()}
    if "nc" not in _NC_CACHE:
        _NC_CACHE["nc"] = build_nc()[0]
    nc = _NC_CACHE["nc"]
    in_maps = [core_inputs(inputs, c) for c in range(8)]
    res = run_bass_kernel_spmd(nc, in_maps, core_ids=list(range(8)))
    out = np.zeros((2, 8192, D), np.float32)
    for c in range(8):
        b, s = c // 4, c % 4
        out[b, s * SEG:(s + 1) * SEG] = res.results[c]["out"]
    return out
```
